# Optimizing a Trainium2 kernel written in Bass

```python
import jax, jax.numpy as jnp
from jax import lax
import numpy as np

D_MODEL = 1024
BATCH = 32
SEQ = 2048
DEPTH = 1

CTX_LEN = 256
GRID_W = 64

A_HEADS = 4
A_DK = 128
A_DV = 128
A_KW = A_HEADS * A_DK
A_VW = A_HEADS * A_DV
A_CHUNK = 64

B_GROUPS = 4
B_GW = 128
B_W = B_GROUPS * B_GW
B_CHUNK = 2 * GRID_W

D_FF = -(-8 * D_MODEL // (3 * 256)) * 256

N_MOD = 6
EPS = 1e-6

CTX_STATE_COLS = 2 * A_KW + A_VW
SPLIT_IDX = (A_KW, 2 * A_KW, 2 * A_KW + A_VW, 3 * A_KW + A_VW, 3 * A_KW + 2 * A_VW,
             3 * A_KW + 2 * A_VW + B_W, 3 * A_KW + 2 * A_VW + 2 * B_W,
             3 * A_KW + 2 * A_VW + 2 * B_W + D_MODEL)
IN_COLS = 3 * A_KW + 2 * A_VW + 2 * B_W + 2 * D_MODEL

kernel_name = 'hybrid_hgrn2_chunkmlp_dit_block'


def rmsnorm(x, g):
    xf = x.astype(jnp.float32)
    y = xf * lax.rsqrt(jnp.mean(xf * xf, axis=-1, keepdims=True) + EPS)
    return (y * g.astype(jnp.float32)).astype(x.dtype)


def layernorm(x, g, b):
    xf = x.astype(jnp.float32)
    mu = jnp.mean(xf, axis=-1, keepdims=True)
    var = jnp.mean(jnp.square(xf - mu), axis=-1, keepdims=True)
    y = (xf - mu) * lax.rsqrt(var + EPS) * g.astype(jnp.float32) + b.astype(jnp.float32)
    return y.astype(x.dtype)


def modulate(h, shift, scale):
    return h * (1 + scale) + shift


def split_heads(t):
    return t.reshape(t.shape[0], t.shape[1], A_HEADS, -1).astype(jnp.float32)


def flip_seq(t):
    return jnp.flip(t, axis=1)


def forget_gate(f_logit, lb):
    f = lb + (1 - lb) * jax.nn.sigmoid(f_logit.astype(jnp.float32))
    return jnp.log(f), 1 - f


def hgrn2_chunked(q, k, logf, v, s0):
    b_, L = q.shape[:2]
    n = L // A_CHUNK
    rs = lambda t: t.reshape(b_, n, A_CHUNK, *t.shape[2:])
    q, k, logf, v = rs(q), rs(k), rs(logf), rs(v)
    bcum = jnp.cumsum(logf, axis=2)
    blast = bcum[:, :, -1:]
    mid = 0.5 * blast
    q_in = q * jnp.exp(bcum - mid)
    k_in = k * jnp.exp(mid - bcum)
    scores = jnp.einsum('bnthd,bnshd->bnhts', q_in, k_in)
    mask = jnp.tril(jnp.ones((A_CHUNK, A_CHUNK), dtype=bool))
    scores = jnp.where(mask, scores, 0.0)
    o_intra = jnp.einsum('bnhts,bnshe->bnthe', scores, v)
    d_state = jnp.einsum('bnshd,bnshe->bnhde', k * jnp.exp(blast - bcum), v)
    decay = jnp.exp(blast[:, :, 0])

    def step(s, inp):
        a, d = inp
        return a[..., None] * s + d, s

    s_final, s_in = lax.scan(step, s0, (jnp.moveaxis(decay, 1, 0), jnp.moveaxis(d_state, 1, 0)))
    s_in = jnp.moveaxis(s_in, 0, 1)
    o_inter = jnp.einsum('bnthd,bnhde->bnthe', q * jnp.exp(bcum), s_in)
    o = (o_intra + o_inter).reshape(b_, L, A_HEADS, A_DV)
    return o, s_final


def hgrn2_final_state(k, logf, v):
    bcum = jnp.cumsum(logf, axis=1)
    w = k * jnp.exp(bcum[:, -1:] - bcum)
    return jnp.einsum('blhd,blhe->bhde', w, v)


def chunk_sgu(u, v, ln_g, ln_b, w_s, b_s, n_chunks):
    b_, L = u.shape[:2]
    v = layernorm(v, ln_g, ln_b)
    vc = v.reshape(b_, n_chunks, B_CHUNK, B_GROUPS, B_GW)
    mixed = jnp.einsum('gts,bnsgc->bntgc', w_s, vc) + jnp.transpose(b_s)[:, :, None]
    return u * mixed.reshape(b_, L, B_W)


def token_mixers(p, lb, g_norm_a, ln_v_g, ln_v_b, w_s, b_s, w_pa, w_pb, w_o, s0_f, s0_b, n_chunks_b):
    f_f, f_b, i_in, q, og, u, v, ga, gb = jnp.split(p, SPLIT_IDX, axis=-1)
    b_, L = p.shape[:2]
    logf_f, k_f = forget_gate(f_f, lb[0])
    logf_b, k_b = forget_gate(f_b, lb[1])
    qh, vh = split_heads(q), split_heads(i_in)
    o_f, s_f = hgrn2_chunked(qh, split_heads(k_f), split_heads(logf_f), vh, s0_f)
    o_b, s_b = hgrn2_chunked(flip_seq(qh), flip_seq(split_heads(k_b)), flip_seq(split_heads(logf_b)),
                             flip_seq(vh), s0_b)
    o_a = rmsnorm(o_f + flip_seq(o_b), g_norm_a).reshape(b_, L, A_VW).astype(p.dtype)
    o_a = o_a * jax.nn.silu(og)
    o_b_mlp = chunk_sgu(jax.nn.gelu(u), jax.nn.gelu(v), ln_v_g, ln_v_b, w_s, b_s, n_chunks_b)
    merged = jax.nn.sigmoid(ga) * (o_a @ w_pa) + jax.nn.sigmoid(gb) * (o_b_mlp @ w_pb)
    return merged @ w_o, s_f, s_b


def swiglu(h, w_up, w_down):
    a, b = jnp.split(h @ w_up, 2, axis=-1)
    return (jax.nn.silu(a) * b) @ w_down


def setup_inputs(seed: int = 0) -> dict:
    key = jax.random.key(seed)
    ks = jax.random.split(key, 21)
    nrm = lambda k, shape, s: jax.random.normal(k, shape, jnp.float32) * s
    return {
        'x': nrm(ks[0], (BATCH, SEQ, D_MODEL), 1.0),
        'c': nrm(ks[1], (BATCH, D_MODEL), 1.0),
        'ctx': nrm(ks[2], (BATCH, CTX_LEN, D_MODEL), 1.0),
        'c_ctx': nrm(ks[3], (D_MODEL,), 1.0),
        'w_mod': nrm(ks[4], (DEPTH, D_MODEL, N_MOD * D_MODEL), 0.5 * D_MODEL ** -0.5),
        'b_mod': nrm(ks[5], (DEPTH, N_MOD * D_MODEL), 0.01),
        'g_mix': 1.0 + nrm(ks[6], (DEPTH, D_MODEL), 0.1),
        'g_ffn': 1.0 + nrm(ks[7], (DEPTH, D_MODEL), 0.1),
        'w_in': nrm(ks[8], (DEPTH, D_MODEL, IN_COLS), D_MODEL ** -0.5),
        'lb_gamma': nrm(ks[9], (DEPTH + 1, 2, A_KW), 0.5),
        'g_norm_a': 1.0 + nrm(ks[10], (DEPTH, A_DV), 0.1),
        'ln_v_g': 1.0 + nrm(ks[11], (DEPTH, B_W), 0.1),
        'ln_v_b': nrm(ks[12], (DEPTH, B_W), 0.02),
        'w_s': nrm(ks[13], (DEPTH, B_GROUPS, B_CHUNK, B_CHUNK), 0.5 * B_CHUNK ** -0.5),
        'b_s': 1.0 + nrm(ks[14], (DEPTH, B_GROUPS, B_CHUNK), 0.1),
        'w_pa': nrm(ks[15], (DEPTH, A_VW, D_MODEL), A_VW ** -0.5),
        'w_pb': nrm(ks[16], (DEPTH, B_W, D_MODEL), B_W ** -0.5),
        'w_o': nrm(ks[17], (DEPTH, D_MODEL, D_MODEL), D_MODEL ** -0.5),
        'w_up': nrm(ks[18], (DEPTH, D_MODEL, 2 * D_FF), D_MODEL ** -0.5),
        'w_down': nrm(ks[19], (DEPTH, D_FF, D_MODEL), D_FF ** -0.5),
        'g_final': 1.0 + nrm(ks[20], (D_MODEL,), 0.1),
    }


def reference(x, c, ctx, c_ctx, w_mod, b_mod, g_mix, g_ffn, w_in, lb_gamma, g_norm_a,
              ln_v_g, ln_v_b, w_s, b_s, w_pa, w_pb, w_o, w_up, w_down, g_final):
    bsz, L = x.shape[0], x.shape[1]
    rows = L // GRID_W
    n_chunks_lat = rows // 2
    n_chunks_ctx = ctx.shape[1] // B_CHUNK
    lb_all = jnp.cumsum(jax.nn.softmax(lb_gamma.astype(jnp.float32), axis=0), axis=0)
    for l in range(DEPTH):
        last = l == DEPTH - 1
        lb = lb_all[l]
        mod = (jax.nn.silu(c) @ w_mod[l] + b_mod[l]).reshape(bsz, N_MOD, D_MODEL)
        mc = (jax.nn.silu(c_ctx) @ w_mod[l] + b_mod[l]).reshape(N_MOD, D_MODEL)
        hc = modulate(rmsnorm(ctx, g_mix[l]), mc[0], mc[1])
        if last:
            pc = hc @ w_in[l][:, :CTX_STATE_COLS]
            f_f, f_b, i_c = jnp.split(pc, (A_KW, 2 * A_KW), axis=-1)
            logf_f, k_f = forget_gate(f_f, lb[0])
            logf_b, k_b = forget_gate(f_b, lb[1])
            vh = split_heads(i_c)
            s_ctx_f = hgrn2_final_state(split_heads(k_f), split_heads(logf_f), vh)
            s_ctx_b = hgrn2_final_state(flip_seq(split_heads(k_b)), flip_seq(split_heads(logf_b)), flip_seq(vh))
        else:
            zeros = jnp.zeros((bsz, A_HEADS, A_DK, A_DV), jnp.float32)
            mix_c, s_ctx_f, s_ctx_b = token_mixers(hc @ w_in[l], lb, g_norm_a[l], ln_v_g[l], ln_v_b[l],
                                                   w_s[l], b_s[l], w_pa[l], w_pb[l], w_o[l],
                                                   zeros, zeros, n_chunks_ctx)
            ctx = ctx + mc[2] * mix_c
            hc2 = modulate(rmsnorm(ctx, g_ffn[l]), mc[3], mc[4])
            ctx = ctx + mc[5] * swiglu(hc2, w_up[l], w_down[l])
        h = modulate(rmsnorm(x, g_mix[l]), mod[:, 0, None], mod[:, 1, None])
        mix_x, _, _ = token_mixers(h @ w_in[l], lb, g_norm_a[l], ln_v_g[l], ln_v_b[l],
                                   w_s[l], b_s[l], w_pa[l], w_pb[l], w_o[l],
                                   s_ctx_f, s_ctx_b, n_chunks_lat)
        x = x + mod[:, 2, None] * mix_x
        h2 = modulate(rmsnorm(x, g_ffn[l]), mod[:, 3, None], mod[:, 4, None])
        x = x + mod[:, 5, None] * swiglu(h2, w_up[l], w_down[l])
    return rmsnorm(x, g_final)
```

```python
import numpy as np
import concourse.bass as bass
import concourse.mybir as mybir
from concourse.bass_utils import run_bass_kernel_spmd

F32 = mybir.dt.float32
BF16 = mybir.dt.bfloat16
AF = mybir.ActivationFunctionType
ALU = mybir.AluOpType

D = 1024
CTX = 256
DFF = 2816
INC = 5632
EPS = 1e-6
NCORES = 8


class Prog:
    ENG = ['pe', 'act', 'dve', 'pool', 'sp']

    def __init__(s, nc):
        s.nc = nc
        s.q = {e: [] for e in s.ENG}
        s.cnt = {}
        s.waited = {e: {} for e in s.ENG}
        s.lastw = {}
        s.rd = {}
        s.sems = {}
        s.owner = None
        s.wown = {}
        s.xown_ok = set()

    def _sem(s, sk):
        if sk not in s.sems:
            s.sems[sk] = s.nc.alloc_semaphore("s_" + "_".join(str(x) for x in sk))
            s.cnt[sk] = 0
        return s.sems[sk]

    def _emit(s, eng, fn, reads, writes, sk, inc):
        deps = {}
        if s.owner is not None:
            for r in reads:
                o = s.wown.get(r)
                base = r
                while isinstance(base, tuple):
                    base = base[0]
                if o is not None and o != s.owner and base not in s.xown_ok:
                    raise RuntimeError("pipeline clobber: key %r written by %r read by %r" % (r, o, s.owner))
        for w in writes:
            s.wown[w] = s.owner

        def add(ev):
            if ev is None:
                return
            k, v = ev
            if deps.get(k, 0) < v:
                deps[k] = v
        for r in reads:
            add(s.lastw.get(r))
        for w in writes:
            add(s.lastw.get(w))
            for k, v in s.rd.get(w, {}).items():
                add((k, v))
        waits = []
        for k, v in deps.items():
            if k[0] == 'd':
                v = s.cnt[k]
            if k == ('e', 'pe') and eng == 'pe':
                continue
            if s.waited[eng].get(k, 0) >= v:
                continue
            s.waited[eng][k] = v
            waits.append((k, v))
        s._sem(sk)
        s.cnt[sk] += inc
        ev = (sk, s.cnt[sk])
        s.q[eng].append((waits, fn, sk, inc))
        for r in reads:
            d = s.rd.setdefault(r, {})
            if d.get(sk, 0) < ev[1]:
                d[sk] = ev[1]
        for w in writes:
            s.lastw[w] = ev
            s.rd[w] = {}
        return ev

    def op(s, eng, fn, reads=(), writes=()):
        return s._emit(eng, fn, reads, writes, ('e', eng), 1)

    def dma(s, q, out, in_, reads=(), writes=(), sem=None, **kw):
        return s._emit(q, lambda e: e.dma_start(out=out, in_=in_, **kw), reads, writes, ('d', sem), 16)

    def barrier(s):
        for eng in s.ENG:
            waits = [(k, v) for k, v in s.cnt.items() if v > 0 and s.waited[eng].get(k, 0) < v]
            for k, v in waits:
                s.waited[eng][k] = v
            s.q[eng].append((waits, None, None, 0))

    def finish(s, eng='sp'):
        waits = [(k, v) for k, v in s.cnt.items() if v > 0 and s.waited[eng].get(k, 0) < v]
        s.q[eng].append((waits, None, None, 0))

    def replay(s):
        nc = s.nc

        def run(name):
            def f(e):
                for waits, fn, sk, inc in s.q[name]:
                    for k, v in waits:
                        e.wait_ge(s.sems[k], v)
                    if fn is not None:
                        fn(e).then_inc(s.sems[sk], inc)
            return f
        with nc.Block() as block:
            block.tensor(run('pe'))
            block.scalar(run('act'))
            block.vector(run('dve'))
            block.gpsimd(run('pool'))
            block.sync(run('sp'))


C_ID = 0
C_AF = 128
C_AB = 256
C_MF = 384
C_MB = 512
C_CH = 640
C_ONE = 642
C_N = 770


def make_consts():
    c = np.zeros((128, C_N), np.float32)
    i = np.arange(128)
    same = (i[:, None] // 64) == (i[None, :] // 64)
    le = (i[:, None] <= i[None, :]) & same
    ge = (i[:, None] >= i[None, :]) & same
    c[:, C_ID:C_ID + 128] = np.eye(128)
    c[:, C_AF:C_AF + 128] = le.astype(np.float32) - 0.5 * same
    c[:, C_AB:C_AB + 128] = ge.astype(np.float32) - 0.5 * same
    c[:, C_MF:C_MF + 128] = le
    c[:, C_MB:C_MB + 128] = ge
    c[:, C_CH] = (i < 64)
    c[:, C_CH + 1] = (i >= 64)
    c[:, C_ONE:C_ONE + 128] = 1.0
    return c


V_BMOD = 0
V_GMIX = 48
V_GFFN = 56
V_LBG = 64
V_GNA = 80
V_BS = 81
V_N = 85


def build(NB, NT, G, use_gelu_tanh=True, P1_SKEW=3, A_SKEW=4):
    import os
    EVS = os.environ.get('EVS', '0') == '1'
    L = NT * 128
    NCT = CTX // 128
    NCH = NT * 2
    nc = bass.Bass("TRN2", target_bir_lowering=False, dynamic_dma_scratch_size=512)
    P = Prog(nc)
    P.xown_ok.update(['Srun', 'sst_d', 'hTg'])

    def din(name, shape, dt=F32):
        return nc.dram_tensor(name, list(shape), dt, kind="ExternalInput").ap()
    x = din("x", [NB * L, D])
    ctx = din("ctx", [NB * CTX, D])
    cvec = din("cvec", [5, D])
    w_mod = din("w_mod", [D, 6 * D])
    vrows = din("vrows", [V_N, 128])
    w_in = din("w_in", [D, INC])
    lnvg = din("ln_v_g", [512])
    lnvb = din("ln_v_b", [512])
    w_s = din("w_s", [4, 128, 128])
    w_pa = din("w_pa", [512, D])
    w_pb = din("w_pb", [512, D])
    w_o = din("w_o", [D, D])
    w_up = din("w_up", [D, INC])
    w_down = din("w_down", [DFF, D])
    g_final = din("g_final", [D])
    cst_d = din("cst", [128, C_N])
    out = nc.dram_tensor("out", [NB * L, D], F32, kind="ExternalOutput").ap()

    def dscr(name, shape, dt=BF16):
        return nc.dram_tensor(name, list(shape), dt, kind="Internal").ap()
    sH = dscr("sH", [128, 8 * 2048])
    sB = dscr("sB", [3, 128, 8 * 512])
    sC = dscr("sC", [8, 128, 3072])
    sD = dscr("sD", [2, 128, 8 * 512])
    sF = dscr("sF", [11, 128, 4096])
    sG = dscr("sG", [4, 128, 22 * 256])
    sst_d = dscr("sst_d", [NB, 2, NCH, 128, 512])

    def sb(name, shape, dt=F32):
        return nc.alloc_sbuf_tensor("sb_" + name, list(shape), dt).ap()

    def pst(name, shape, dt=F32):
        return nc.alloc_psum_tensor("ps_" + name, list(shape), dt).ap()

    T = G * 128
    cst = sb("cst", [128, C_N])
    identf = cst[:, C_ID:C_ID + 128]
    onesf = cst[:, C_ONE:C_ONE + 128]
    identb = sb("identb", [128, 128], BF16)
    maskb = sb("maskb", [128, 2, 128], BF16)
    vc = sb("vc", [128, V_N])
    modc = sb("modc", [128, 48, 5])
    A1 = sb("A1", [128, 8, 5])
    A2 = sb("A2", [128, 8, 5])
    omlc = sb("omlc", [128, 8])
    OML = sb("OML", [128, 1024])
    G2 = sb("G2", [128, 1024])
    G5 = sb("G5", [128, 1024])
    gfin = sb("gfin", [128, 1024])
    lng = sb("lng", [128, 512])
    lnb = sb("lnb", [128, 512])
    wsT = sb("wsT", [128, 4, 128], BF16)
    scT = sb("scT", [128, 8, 5])
    win_h = sb("win_h", [128, 8, 2048], BF16)
    NSLOT = 3
    SLOTN = 5632
    ring = [sb("ring%d" % i, [128, SLOTN], BF16) for i in range(NSLOT)]
    stgf = [sb("stgf%d" % i, [128, 2048]) for i in range(2)]
    Rt = stgf[0][:, 1024:2048].rearrange("p (a b) -> p a b", a=8)
    Srun = sb("Srun", [128, 2, 512])
    xgs = [sb("xg%d" % i, [128, G, 1024]) for i in range(2)]
    hTg = sb("hTg", [128, 8, T], BF16)
    xt1 = [stgf[1][:, i * 1024:(i + 1) * 1024] for i in range(2)]
    hT1 = [sb("hT1_%d" % i, [128, 8, 128], BF16) for i in range(2)]
    xn2 = [sb("xn%d" % i, [128, 1024], BF16) for i in range(2)]
    ss = sb("ss", [128, 24])
    sf = sb("sf", [128, 1024])
    kf = sb("kf", [128, 1024])
    lf = sb("lf", [128, 1024])
    Ep = sb("Ep", [128, 1024])
    Em = sb("Em", [128, 1024])
    t512 = [sf[:, 0:512], sf[:, 512:1024], Ep[:, 0:512], Ep[:, 512:1024]]
    T5K = {0: ('sf', 0), 1: ('sf', 1), 2: ('Ep', 0), 3: ('Ep', 1)}
    assert T <= 256
    gt = [Em[:, i * 256:i * 256 + T] for i in range(4)]
    GTK = {i: ('Em', i // 2) for i in range(4)}
    gtsets = [gt, [kf[:, i * 256:i * 256 + T] for i in range(4)]]
    gtkeys = [GTK, {i: ('kf', i // 2) for i in range(4)}]
    gtF = [sb("gtF%d" % i, [128, T]) for i in range(2)]
    kiT = sb("kiT", [128, 8, 128], BF16)
    qiT = sb("qiT", [128, 8, 128], BF16)
    kib2 = [sb("kib%d" % i, [128, 512], BF16) for i in range(2)]
    vb2 = [sb("vb%d" % i, [128, 512], BF16) for i in range(2)]
    vbA2 = [sb("vbA%d" % i, [128, 512], BF16) for i in range(2)]
    expm2 = [sb("expm%d" % i, [128, 4, 2]) for i in range(2)]
    decay2 = [sb("decay%d" % i, [128, 4, 2]) for i in range(2)]
    tmpS2 = [sb("tmpS%d" % i, [128, 512]) for i in range(2)]
    sstw = [sb("sstw%d" % i, [128, 512], BF16) for i in range(4)]
    sstr2 = [sb("sstr%d" % i, [128, 2, 2, 512], BF16) for i in range(2)]
    scm = sb("scm", [128, 2, 4, 128], BF16)
    oa = sb("oa", [128, G, 512])
    rs4 = sb("rs4", [128, 8])
    oag2 = [sb("oag%d" % i, [128, 512], BF16) for i in range(2)]
    gu = sb("gu", [128, G, 512])
    vnb2 = [sb("vnb%d" % i, [128, 512], BF16) for i in range(2)]
    obm2 = [sb("obm%d" % i, [128, 512], BF16) for i in range(2)]
    bnst = sb("bnst", [128, 8])
    oaT = sb("oaT", [128, 4, T], BF16)
    obT = sb("obT", [128, 4, T], BF16)
    mT = sb("mT", [128, 8, T], BF16)
    actT = sb("actT", [128, 22, T], BF16)
    yt = [stgf[0][:, 0:1024]] * 2
    pbank = [pst("pb%d" % i, [128, 512]) for i in range(8)]
    pbf = [pbank[i].bitcast(BF16) for i in range(8)]

    pk = lambda i: ('ps', i)

    P.dma('sp', cst, cst_d, writes=['cst'], sem='cst')
    P.op('dve', lambda e: e.tensor_copy(out=identb, in_=identf), reads=['cst'], writes=['identb'])
    P.op('dve', lambda e: e.tensor_copy(out=maskb[:, 0, :], in_=cst[:, C_MF:C_MF + 128]), reads=['cst'], writes=['maskb'])
    P.op('dve', lambda e: e.tensor_copy(out=maskb[:, 1, :], in_=cst[:, C_MB:C_MB + 128]), reads=['cst'], writes=['maskb'])
    P.dma('sp', gfin, g_final.partition_broadcast(128), writes=['gfin'], sem='c1')
    P.dma('sp', lng, lnvg.partition_broadcast(128), writes=['lng'], sem='c1')
    P.dma('sp', lnb, lnvb.partition_broadcast(128), writes=['lnb'], sem='c1')

    cast_i = [0]

    def precast(wsrc, KC, colw, dests, late=False, sset=0, c_lo=0, c_hi=None):
        ncols = wsrc.shape[1] if c_hi is None else c_hi
        src = wsrc.rearrange("(k p) c -> p k c", p=128)
        for c0 in range(c_lo, ncols, colw):
            cast_i[0] += 1
            if late:
                i = sset
                if i == 0:
                    fbuf = actT.rearrange("p a b -> p (a b)").bitcast(F32)
                    bbuf = mT.rearrange("p a b -> p (a b)")
                    fk = [('actT', jf) for jf in range(22)]
                    bk_ = [('mT', jb) for jb in range(8)]
                else:
                    fbuf = xgs[1].rearrange("p a b -> p (a b)")
                    bbuf = hTg.rearrange("p a b -> p (a b)")
                    fk = [('xg', 1, jj) for jj in range(G)]
                    bk_ = [(('hTg', jj), blk) for jj in range(G) for blk in range(8)]
                sfv = fbuf[:, 0:KC * colw].rearrange("p (k c) -> p k c", k=KC)
                sbv = bbuf[:, 0:KC * colw].rearrange("p (k c) -> p k c", k=KC)
                eng = ['dve', 'pool'][cast_i[0] % 2]
                semi, semo = ('lstg_i', i), ('lstg_o', i)
            else:
                i = cast_i[0] % 2
                sfv = stgf[i][:, 0:KC * colw].rearrange("p (k c) -> p k c", k=KC)
                sbv = ring[i][:, 0:KC * colw].rearrange("p (k c) -> p k c", k=KC)
                fk = [('stgf', i)]
                bk_ = [('ring', i)]
                eng = ['dve', 'pool', 'act'][cast_i[0] % 3]
                semi, semo = ('stgf', i), ('cst_o', i)
            P.dma('sp', sfv, src[:, :, c0:c0 + colw], writes=fk, sem=semi)
            if late:
                yield
            if eng == 'act':
                P.op('act', lambda e, a=sbv, b=sfv: e.activation(out=a, in_=b, func=AF.Copy), reads=fk, writes=bk_)
            else:
                P.op(eng, lambda e, a=sbv, b=sfv: e.tensor_copy(out=a, in_=b), reads=fk, writes=bk_)
            if late:
                yield
            for dst, off, w, key in dests(c0):
                P.dma('sp', dst, sbv[:, :, off:off + w], reads=bk_, writes=[key], sem=semo)
            yield

    def k3(ap2d, k):
        return ap2d.rearrange("p (k c) -> p k c", k=k)

    def d_win(c0):
        if c0 < 2048:
            return [(k3(sH, 8)[:, :, c0:c0 + 256], 0, 256, 'sH')]
        if c0 < 3584:
            p_, o_ = (c0 - 2048) // 512, (c0 - 2048) % 512
            return [(k3(sB[p_], 8)[:, :, o_:o_ + 256], 0, 256, ('sB', p_))]
        r = []
        for hb in range(2):
            c = c0 + hb * 128
            if c < 4608:
                j = (c - 3584) // 128
                r.append((k3(sC[j][:, 0:1024], 8), hb * 128, 128, ('sC', j)))
            else:
                j = (c - 4608) // 128
                r.append((k3(sC[j][:, 1024:2048], 8), hb * 128, 128, ('sC', j)))
        return r

    def d_wp(base):
        def f(c0):
            return [(k3(sC[c0 // 128 + q][:, base:base + 512], 4), q * 128, 128, ('sC', c0 // 128 + q)) for q in range(4)]
        return f

    def d_wo(c0):
        hf, o_ = c0 // 512, c0 % 512
        return [(k3(sD[hf], 8)[:, :, o_:o_ + 256], 0, 256, ('sD', hf))]

    def d_wup(c0):
        r = []
        for hb in range(2):
            c = c0 + hb * 128
            if c < DFF:
                j = c // 128
                o2 = (j % 2) * 2048
                r.append((k3(sF[j // 2][:, o2:o2 + 1024], 8), hb * 128, 128, ('sF', j // 2)))
            else:
                j = (c - DFF) // 128
                o2 = (j % 2) * 2048 + 1024
                r.append((k3(sF[j // 2][:, o2:o2 + 1024], 8), hb * 128, 128, ('sF', j // 2)))
        return r

    def d_wdn(c0):
        q4, o_ = c0 // 256, c0 % 256
        return [(k3(sG[q4], 22)[:, :, o_:o_ + 64], 0, 64, ('sG', q4))]

    for _ in precast(w_in, 8, 256, d_win):
        pass

    def late_precast_a():
        yield from precast(w_pa, 4, 512, d_wp(2048), late=True, sset=0)
        yield from precast(w_pb, 4, 512, d_wp(2560), late=True, sset=0)
        yield from precast(w_o, 8, 256, d_wo, late=True, sset=0)
        yield from precast(w_up, 8, 256, d_wup, late=True, sset=0, c_lo=0, c_hi=3840)

    def late_precast_b():
        yield from precast(w_up, 8, 256, d_wup, late=True, sset=1, c_lo=3840)
        yield from precast(w_down, 22, 64, d_wdn, late=True, sset=1)

    for q in range(4):
        P.dma('sp', win_h.rearrange("p k c -> p (k c)")[:, q * 4096:(q + 1) * 4096], sH[:, q * 4096:(q + 1) * 4096], reads=['sH'], writes=['win_h'], sem='win_h')

    c5 = sf[0:5, :]
    P.dma('sp', c5, cvec, writes=['sf'], sem='c5')
    P.op('act', lambda e: e.activation(out=c5, in_=c5, func=AF.Silu), reads=['sf'], writes=['sf'])
    for k in range(8):
        P.op('pe', lambda e, k=k: e.transpose(out=pbank[0][:, k * 5:(k + 1) * 5], in_=sf[0:5, k * 128:(k + 1) * 128], identity=identf[0:5, 0:5]),
             reads=['sf', 'cst'], writes=[pk(0)])
    P.op('dve', lambda e: e.tensor_copy(out=scT.rearrange("p k b -> p (k b)"), in_=pbank[0][:, 0:40]), reads=[pk(0)], writes=['scT'])
    vr = kf[0:V_N, 0:128]
    P.dma('sp', vr, vrows, writes=['kf'], sem='vr')
    P.op('pe', lambda e: e.transpose(out=pbank[1][:, 0:V_N], in_=vr, identity=identf[0:V_N, 0:V_N]), reads=['kf', 'cst'], writes=[pk(1)])
    P.op('dve', lambda e: e.tensor_copy(out=vc, in_=pbank[1][:, 0:V_N]), reads=[pk(1)], writes=['vc'])
    wmv = w_mod.rearrange("(k p) c -> p k c", p=128)
    for cg in range(24):
        i = cg % 2
        wv = stgf[i].rearrange("p (k c) -> p k c", k=8)
        P.dma('sp', wv, wmv[:, :, cg * 256:(cg + 1) * 256], writes=[('stgf', i)], sem=('stgf', i))
        for bl in range(2):
            blk = cg * 2 + bl
            for k in range(8):
                P.op('pe', lambda e, wv=wv, k=k, bl=bl, blk=blk: e.matmul(out=pbank[2][:, blk * 5:(blk + 1) * 5], lhsT=wv[:, k, bl * 128:(bl + 1) * 128],
                                                                       rhs=scT[:, k, :], start=(k == 0), stop=(k == 7)),
                     reads=[('stgf', i), 'scT'], writes=[pk(2)])
    P.op('dve', lambda e: e.tensor_tensor(out=modc, in0=pbank[2][:, 0:240].rearrange("p (a b) -> p a b", b=5),
                                          in1=vc[:, V_BMOD:V_BMOD + 48].unsqueeze(2).to_broadcast([128, 48, 5]), op=ALU.add),
         reads=[pk(2), 'vc'], writes=['modc'])
    P.op('dve', lambda e: e.scalar_tensor_tensor(out=A1, in0=modc[:, 8:16, :], scalar=1.0, in1=vc[:, V_GMIX:V_GMIX + 8].unsqueeze(2).to_broadcast([128, 8, 5]),
                                                 op0=ALU.add, op1=ALU.mult), reads=['modc', 'vc'], writes=['A1'])
    P.op('dve', lambda e: e.scalar_tensor_tensor(out=A2, in0=modc[:, 32:40, :], scalar=1.0, in1=vc[:, V_GFFN:V_GFFN + 8].unsqueeze(2).to_broadcast([128, 8, 5]),
                                                 op0=ALU.add, op1=ALU.mult), reads=['modc', 'vc'], writes=['A2'])
    P.op('dve', lambda e: e.tensor_tensor(out=omlc, in0=vc[:, V_LBG + 8:V_LBG + 16], in1=vc[:, V_LBG:V_LBG + 8], op=ALU.subtract), reads=['vc'], writes=['omlc'])
    P.op('act', lambda e: e.activation(out=omlc, in_=omlc, func=AF.Sigmoid), reads=['omlc'], writes=['omlc'])

    def bcast_tile(dst, colsrc_fn, rkeys, dkey):
        P.op('dve', lambda e: e.tensor_tensor(out=Rt, in0=identf.unsqueeze(1).to_broadcast([128, 8, 128]),
                                              in1=colsrc_fn().unsqueeze(2).to_broadcast([128, 8, 128]), op=ALU.mult),
             reads=['cst'] + rkeys, writes=['Rt'])
        for hf in range(2):
            P.op('pe', lambda e, hf=hf: e.matmul(out=pbank[3 + hf], lhsT=onesf, rhs=Rt[:, hf * 4:(hf + 1) * 4, :].rearrange("p a b -> p (a b)"), start=True, stop=True),
                 reads=['Rt', 'cst'], writes=[pk(3 + hf)])
            P.op('act', lambda e, hf=hf: e.activation(out=dst[:, hf * 512:(hf + 1) * 512], in_=pbank[3 + hf], func=AF.Copy), reads=[pk(3 + hf)], writes=[dkey])

    P.barrier()
    bcast_tile(OML, lambda: omlc, ['omlc'], 'OML')
    wsf = lf[:, 0:512].rearrange("p (g s) -> p g s", g=4)
    P.dma('sp', wsf, w_s.rearrange("g t s -> t g s"), writes=['lf'], sem='wsf')
    for g4 in range(4):
        P.op('pe', lambda e, g4=g4: e.transpose(out=pbank[5][:, g4 * 128:(g4 + 1) * 128], in_=wsf[:, g4, :], identity=identf), reads=['lf', 'cst'], writes=[pk(5)])
    P.op('dve', lambda e: e.tensor_copy(out=wsT.rearrange("p g t -> p (g t)"), in_=pbank[5]), reads=[pk(5)], writes=['wsT'])

    def front(src_ap, xbuf, xkey, Acols, Bcols, hT_out, hkey, load=True, sem=None, slot=0, bank=0):
        c0 = slot * 3
        k0, k1, k2 = ('ss', slot, 0), ('ss', slot, 1), ('ss', slot, 2)
        xn_ = xn2[slot % 2]
        xnk = ('xn', slot % 2)
        if load:
            P.dma('sp', xbuf, src_ap, writes=[xkey], sem=sem)
        P.op('act', lambda e: e.activation(out=xn_, in_=xbuf, func=AF.Square, accum_out=ss[:, c0:c0 + 1]), reads=[xkey], writes=[k0, xnk])
        P.op('act', lambda e: e.activation(out=ss[:, c0 + 1:c0 + 2], in_=ss[:, c0:c0 + 1], func=AF.Ln, scale=1.0 / D, bias=EPS_AP),
             reads=[k0, 'eps'], writes=[k1])
        P.op('act', lambda e: e.activation(out=ss[:, c0 + 2:c0 + 3], in_=ss[:, c0 + 1:c0 + 2], func=AF.Exp, scale=-0.5), reads=[k1], writes=[k2])
        P.op('dve', lambda e: e.tensor_scalar(out=xn_, in0=xbuf, scalar1=ss[:, c0 + 2:c0 + 3], scalar2=None, op0=ALU.mult), reads=[xkey, k2], writes=[xnk])
        yield
        for blk in range(8):
            P.op('pe', lambda e, blk=blk: e.transpose(out=pbf[bank][:, blk * 128:(blk + 1) * 128], in_=xn_[:, blk * 128:(blk + 1) * 128], identity=identb),
                 reads=[xnk, 'identb'], writes=[pk(bank)])
        yield
        for blk in range(8):
            if blk % 2 == 0 or not EVS:
                P.op('act', lambda e, blk=blk: e.activation(out=hT_out[:, blk, :], in_=pbf[bank][:, blk * 128:(blk + 1) * 128], func=AF.Identity,
                                                            scale=Acols(blk), bias=Bcols(blk)),
                     reads=[pk(bank), 'A1', 'A2', 'modc'], writes=[(hkey, blk)])
            else:
                P.op('dve', lambda e, blk=blk: e.tensor_scalar(out=hT_out[:, blk, :], in0=pbf[bank][:, blk * 128:(blk + 1) * 128], scalar1=Acols(blk), scalar2=Bcols(blk),
                                                               op0=ALU.mult, op1=ALU.add),
                     reads=[pk(bank), 'A1', 'A2', 'modc'], writes=[(hkey, blk)])
        yield

    def drive_first(bgs, gens, skew):
        bgs = list(bgs)
        pending = list(gens)
        active = []
        tick = 0
        while pending or active or bgs:
            if pending and tick % skew == 0:
                for _ in range(2):
                    if pending:
                        active.append(pending.pop(0))
            for bg in list(bgs):
                try:
                    P.owner = None
                    next(bg)
                except StopIteration:
                    bgs.remove(bg)
            for g_ in list(active):
                try:
                    P.owner = id(g_)
                    next(g_)
                except StopIteration:
                    active.remove(g_)
                P.owner = None
            tick += 1

    def HK(hkey):
        return [(hkey, blk) for blk in range(8)]

    def drive(gens, skew, per_start=1):
        pending = list(gens)
        active = []
        tick = 0
        while pending or active:
            if pending and tick % skew == 0:
                for _ in range(per_start):
                    if pending:
                        active.append(pending.pop(0))
            for g_ in list(active):
                try:
                    P.owner = id(g_)
                    next(g_)
                except StopIteration:
                    active.remove(g_)
                P.owner = None
            tick += 1

    epsT = sb("epsT", [128, 1])
    P.op('pool', lambda e: e.memset(epsT, EPS), writes=['eps'])
    EPS_AP = epsT[:, 0:1]

    def proj_tok(bank, hT, hkey, wcols, wkey, ncol=512):
        for k in range(8):
            P.op('pe', lambda e, k=k, r_=wcols(k): e.matmul(out=pbank[bank][:, 0:ncol], lhsT=hT[:, k, :], rhs=r_, start=(k == 0), stop=(k == 7)),
                 reads=HK(hkey) + [wkey], writes=[pk(bank)])

    def ffront(zbanks, ndir, dir0, half0=0, part=None):
        its = [(i, dir0 + i, half0 + i, slice((half0 + i) * 512, (half0 + i + 1) * 512)) for i in range(ndir)]
        for i, d, hh, sl in (its if part in (None, 0) else []):
            P.op('act', lambda e, i=i, sl=sl: e.activation(out=sf[:, sl], in_=pbank[zbanks[i]], func=AF.Exp), reads=[pk(zbanks[i])], writes=[('sf', hh)])
        for i, d, hh, sl in (its if part in (None, 0) else []):
            P.op('dve', lambda e, sl=sl: e.tensor_scalar(out=sf[:, sl], in0=sf[:, sl], scalar1=1.0, scalar2=None, op0=ALU.add), reads=[('sf', hh)], writes=[('sf', hh)])
            P.op('dve', lambda e, sl=sl: e.reciprocal(out=sf[:, sl], in_=sf[:, sl]), reads=[('sf', hh)], writes=[('sf', hh)])
            P.op('dve', lambda e, sl=sl, d=d: e.tensor_tensor(out=kf[:, sl], in0=sf[:, sl], in1=OML[:, d * 512:(d + 1) * 512], op=ALU.mult),
                 reads=[('sf', hh), 'OML'], writes=[('kf', hh)])
        for i, d, hh, sl in (its if part in (None, 1) else []):
            P.op('act', lambda e, sl=sl: e.activation(out=lf[:, sl], in_=kf[:, sl], func=AF.Ln, scale=-1.0, bias=ONE_AP), reads=[('kf', hh), 'eps'], writes=[('lf', hh)])

    oneT = sb("oneT", [128, 1])
    P.op('pool', lambda e: e.memset(oneT, 1.0), writes=['eps'])
    ONE_AP = oneT[:, 0:1]

    def p1_tile(b, dirn, kind, ti, step):
        i2 = dirn
        B0, B1, B2, B3 = 4 * dirn, 4 * dirn + 1, 4 * dirn + 2, 4 * dirn + 3
        hsl = slice(dirn * 512, (dirn + 1) * 512)
        kib_, vb_, expm_, decay_, tmpS_ = kib2[dirn], vb2[dirn], expm2[dirn], decay2[dirn], tmpS2[dirn]
        kk = lambda n: (n, dirn)
        if kind == 'c':
            src = ctx[b * CTX + ti * 128: b * CTX + (ti + 1) * 128, :]
            mi = 4
        else:
            src = x[b * L + ti * 128: b * L + (ti + 1) * 128, :]
            mi = b
        yield from front(src, xt1[i2], ('xt1', i2), lambda blk: A1[:, blk, mi:mi + 1], lambda blk: modc[:, blk, mi:mi + 1], hT1[i2], ('hT1', i2),
                         sem=('xt1', i2), slot=dirn, bank=B0)
        hk = ('hT1', i2)
        proj_tok(B0, hT1[i2], hk, lambda k: win_h[:, k, dirn * 512:(dirn + 1) * 512], 'win_h')
        proj_tok(B3, hT1[i2], hk, lambda k: win_h[:, k, 1024:1536], 'win_h')
        yield
        ffront([B0], 1, dirn, half0=dirn)
        P.op('dve', lambda e: e.tensor_copy(out=vb_, in_=pbank[B3]), reads=[pk(B3)], writes=[kk('vb')])
        yield
        acol = C_AF if dirn == 0 else C_AB
        P.op('pe', lambda e: e.matmul(out=pbank[B1], lhsT=cst[:, acol:acol + 128], rhs=lf[:, hsl], start=True, stop=True), reads=[('lf', dirn), 'cst'], writes=[pk(B1)])
        for h in range(4):
            P.op('pe', lambda e, h=h: e.matmul(out=pbank[B2][:, h * 2:(h + 1) * 2], lhsT=lf[:, dirn * 512 + h * 128:dirn * 512 + (h + 1) * 128], rhs=cst[:, C_CH:C_CH + 2],
                                               start=True, stop=True),
                 reads=[('lf', dirn), 'cst'], writes=[pk(B2)])
        yield
        P.op('act', lambda e: e.activation(out=Em[:, hsl], in_=pbank[B1], func=AF.Exp, scale=-1.0), reads=[pk(B1)], writes=[('Em', dirn)])
        P.op('dve', lambda e: e.tensor_tensor(out=kib_, in0=kf[:, hsl], in1=Em[:, hsl], op=ALU.mult), reads=[('kf', dirn), ('Em', dirn)], writes=[kk('kib')])
        P.op('act', lambda e: e.activation(out=expm_.rearrange("p a b -> p (a b)"), in_=pbank[B2][:, 0:8], func=AF.Exp, scale=0.5), reads=[pk(B2)], writes=[kk('expm')])
        P.op('act', lambda e: e.activation(out=decay_.rearrange("p a b -> p (a b)"), in_=pbank[B2][:, 0:8], func=AF.Exp), reads=[pk(B2)], writes=[kk('decay')])
        yield
        DB = [B1, B2]
        for ch in range(2):
            for h in range(4):
                P.op('pe', lambda e, ch=ch, h=h: e.matmul(out=pbank[DB[ch]][:, h * 128:(h + 1) * 128], lhsT=kib_[ch * 64:(ch + 1) * 64, h * 128:(h + 1) * 128],
                                                          rhs=vb_[ch * 64:(ch + 1) * 64, h * 128:(h + 1) * 128], start=True, stop=True),
                     reads=[kk('kib'), kk('vb')], writes=[pk(DB[ch])])
        yield
        Sr = Srun[:, dirn, :]
        Sr3 = Sr.rearrange("p (h e) -> p h e", h=4)
        skeys = [('Srun', dirn, h) for h in range(4)]
        for ch in ([0, 1] if dirn == 0 else [1, 0]):
            em_bc = expm_[:, :, ch:ch + 1].to_broadcast([128, 4, 128])
            if kind == 'l':
                n = ti * 2 + ch
                wi = dirn * 2 + ch
                P.op('pool', lambda e, wi=wi, em_bc=em_bc: e.tensor_tensor(out=sstw[wi].rearrange("p (h e) -> p h e", h=4), in0=Sr3, in1=em_bc, op=ALU.mult),
                     reads=skeys + [kk('expm')], writes=[('sstw', wi)])
                P.dma('sp', sst_d[b, dirn, n], sstw[wi], reads=[('sstw', wi)], writes=[('sst_d', b, dirn, n)], sem=('sstw', wi))
            for h in range(4):
                hs = slice(h * 128, (h + 1) * 128)
                P.op('act', lambda e, ch=ch, h=h, hs=hs: e.activation(out=tmpS_[:, hs], in_=pbank[DB[ch]][:, hs], func=AF.Identity, scale=expm_[:, h, ch:ch + 1], bias=0.0),
                     reads=[pk(DB[ch]), kk('expm')], writes=[(kk('tmpS'), h)])
            for h in range(4):
                hs = slice(h * 128, (h + 1) * 128)
                P.op('dve', lambda e, ch=ch, h=h, hs=hs: e.scalar_tensor_tensor(out=Sr[:, hs], in0=Sr[:, hs], scalar=decay_[:, h, ch:ch + 1], in1=tmpS_[:, hs],
                                                                             op0=ALU.mult, op1=ALU.add),
                     reads=[skeys[h], kk('decay'), (kk('tmpS'), h)], writes=[skeys[h]])
        yield

    pieces = []
    ring_state = {'issued': 0, 'used': 0}

    def ring_issue_upto(n):
        while ring_state['issued'] < min(n, len(pieces)):
            i = ring_state['issued']
            pieces[i](i % NSLOT)
            ring_state['issued'] += 1

    def ring_next(hold=0):
        i = ring_state['used']
        ring_issue_upto(i + NSLOT - hold)
        ring_state['used'] += 1
        return i % NSLOT

    def ld(slot, view, src, rkey):
        P.dma('sp', view, src, reads=[rkey], writes=[('ring', slot)], sem=('ring', slot))

    def v3(slot, off, k, c):
        return ring[slot][:, off:off + k * c].rearrange("p (k c) -> p k c", k=k)

    def group_pieces():
        for p_ in (2, 0, 1):
            pieces.append(lambda s, p_=p_: ld(s, ring[s][:, 0:4096], sB[p_], ('sB', p_)))
        for j in range(8):
            pieces.append(lambda s, j=j: ld(s, ring[s][:, 0:3072], sC[j], ('sC', j)))
        for hf in range(2):
            pieces.append(lambda s, hf=hf: ld(s, ring[s][:, 0:4096], sD[hf], ('sD', hf)))
        for j in range(11):
            pieces.append(lambda s, j=j: ld(s, ring[s][:, 0:4096], sF[j], ('sF', j)))
        for q4 in range(4):
            pieces.append(lambda s, q4=q4: ld(s, ring[s][:, 0:5632], sG[q4], ('sG', q4)))

    for b in range(NB):
        for g in range(NT // G):
            group_pieces()

    def gelu_from_psum(bank, dst, dkey, tmps):
        ps = pbank[bank]
        if use_gelu_tanh:
            P.op('act', lambda e: e.activation(out=dst, in_=ps, func=AF.Gelu_apprx_tanh), reads=[pk(bank)], writes=[dkey])
            return
        t0, t1 = tmps
        P.op('act', lambda e: e.activation(out=t512[t0], in_=ps, func=AF.Square), reads=[pk(bank)], writes=[T5K[t0]])
        P.op('dve', lambda e: e.tensor_scalar(out=t512[t0], in0=t512[t0], scalar1=0.044715, scalar2=1.0, op0=ALU.mult, op1=ALU.add), reads=[T5K[t0]], writes=[T5K[t0]])
        P.op('dve', lambda e: e.tensor_tensor(out=t512[t0], in0=t512[t0], in1=ps, op=ALU.mult), reads=[T5K[t0], pk(bank)], writes=[T5K[t0]])
        P.op('act', lambda e: e.activation(out=t512[t1], in_=t512[t0], func=AF.Sigmoid, scale=1.5957691216057308), reads=[T5K[t0]], writes=[T5K[t1]])
        P.op('dve', lambda e: e.tensor_tensor(out=dst, in0=t512[t1], in1=ps, op=ALU.mult), reads=[T5K[t1], pk(bank)], writes=[dkey])

    def load_xg(b, g):
        for jj in range(G):
            j = g * G + jj
            P.dma('sp', xgs[g % 2][:, jj, :], x[b * L + j * 128: b * L + (j + 1) * 128, :], writes=[('xg', g % 2, jj)], sem=('xg', g % 2, jj))

    def p2_group(b, g, fronts_done=False):
        xg = xgs[g % 2]
        xgk = lambda jj: ('xg', g % 2, jj)
        def stageA_tile(jj):
            j = g * G + jj
            tsl = slice(jj * 128, (jj + 1) * 128)
            hTt = hTg[:, :, tsl]
            hk = ('hTg', jj)
            xk = xgk(jj)
            sstr = sstr2[jj % 2]
            vbA = vbA2[jj % 2]
            vbk = ('vbA', jj % 2)
            sk_ = ('sstr', jj % 2)
            if not fronts_done:
                yield from front(None, xg[:, jj, :], xk, lambda blk: A1[:, blk, b:b + 1], lambda blk: modc[:, blk, b:b + 1], hTt, hk, load=False, slot=2 + jj % 2, bank=0)
            for dirn in range(2):
                for ch in range(2):
                    P.dma('sp', sstr[:, dirn, ch, :], sst_d[b, dirn, j * 2 + ch], reads=[('sst_d', b, dirn, j * 2 + ch)], writes=[sk_], sem=sk_)
            proj_tok(1, hTt, hk, lambda k: win_h[:, k, 0:512], 'win_h')
            proj_tok(2, hTt, hk, lambda k: win_h[:, k, 512:1024], 'win_h')
            ffront([1, 2], 2, 0)
            proj_tok(3, hTt, hk, lambda k: win_h[:, k, 1024:1536], 'win_h')
            P.op('dve', lambda e: e.tensor_copy(out=vbA, in_=pbank[3]), reads=[pk(3)], writes=[vbk])
            yield
            for h in range(4):
                for k in range(8):
                    P.op('pe', lambda e, h=h, k=k, hTt=hTt: e.matmul(out=pbank[1][:, h * 128:(h + 1) * 128], lhsT=win_h[:, k, 1536 + h * 128:1536 + (h + 1) * 128], rhs=hTt[:, k, :],
                                                            start=(k == 0), stop=(k == 7)), reads=HK(hk) + ['win_h'], writes=[pk(1)])
            for blk in range(8):
                P.op('pe', lambda e, blk=blk: e.transpose(out=pbank[6 + blk // 4][:, (blk % 4) * 128:(blk % 4 + 1) * 128], in_=kf[:, blk * 128:(blk + 1) * 128], identity=identf),
                     reads=[('kf', blk // 4), 'cst'], writes=[pk(6 + blk // 4)])
            for blk in range(8):
                acol = C_AF if blk < 4 else C_AB
                P.op('pe', lambda e, blk=blk, acol=acol: e.matmul(out=pbank[4 + blk // 4][:, (blk % 4) * 128:(blk % 4 + 1) * 128], lhsT=lf[:, blk * 128:(blk + 1) * 128],
                                                                  rhs=cst[:, acol:acol + 128], start=True, stop=True),
                     reads=[('lf', blk // 4), 'cst'], writes=[pk(4 + blk // 4)])
            yield
            for d2 in range(2):
                sl = slice(d2 * 512, (d2 + 1) * 512)
                P.op('act', lambda e, d2=d2, sl=sl: e.activation(out=Ep[:, sl], in_=pbank[4 + d2], func=AF.Exp), reads=[pk(4 + d2)], writes=[('Ep', d2)])
                P.op('act', lambda e, d2=d2, sl=sl: e.activation(out=Em[:, sl], in_=pbank[4 + d2], func=AF.Exp, scale=-1.0), reads=[pk(4 + d2)], writes=[('Em', d2)])
                P.op('dve', lambda e, d2=d2, sl=sl: e.tensor_tensor(out=kiT[:, d2 * 4:(d2 + 1) * 4, :].rearrange("p a b -> p (a b)"), in0=pbank[6 + d2], in1=Em[:, sl], op=ALU.mult),
                     reads=[pk(6 + d2), ('Em', d2)], writes=[('kiT', d2)])
                P.op('dve', lambda e, d2=d2, sl=sl: e.tensor_tensor(out=qiT[:, d2 * 4:(d2 + 1) * 4, :].rearrange("p a b -> p (a b)"), in0=pbank[1], in1=Ep[:, sl], op=ALU.mult),
                     reads=[pk(1), ('Ep', d2)], writes=[('qiT', d2)])
            yield
            for d2 in range(2):
                for h in range(4):
                    P.op('pe', lambda e, d2=d2, h=h: e.matmul(out=pbank[2 + d2][:, h * 128:(h + 1) * 128], lhsT=kiT[:, d2 * 4 + h, :], rhs=qiT[:, d2 * 4 + h, :], start=True, stop=True),
                         reads=[('kiT', d2), ('qiT', d2)], writes=[pk(2 + d2)])
                P.op('dve', lambda e, d2=d2: e.tensor_tensor(out=scm[:, d2, :, :], in0=pbank[2 + d2].rearrange("p (h t) -> p h t", h=4),
                                                            in1=maskb[:, d2:d2 + 1, :].to_broadcast([128, 4, 128]), op=ALU.mult),
                     reads=[pk(2 + d2), 'maskb'], writes=[('scm', d2)])
            yield
            for h in range(4):
                ob_ = pbank[4][:, h * 128:(h + 1) * 128]
                P.op('pe', lambda e, h=h, ob_=ob_: e.matmul(out=ob_, lhsT=scm[:, 0, h, :], rhs=vbA[:, h * 128:(h + 1) * 128], start=True, stop=False),
                     reads=[('scm', 0), vbk], writes=[pk(4)])
                for d2 in range(2):
                    for ch in range(2):
                        P.op('pe', lambda e, h=h, d2=d2, ch=ch, sstr=sstr: e.matmul(out=pbank[4][ch * 64:(ch + 1) * 64, h * 128:(h + 1) * 128],
                                                                        lhsT=qiT[:, d2 * 4 + h, ch * 64:(ch + 1) * 64],
                                                                        rhs=sstr[:, d2, ch, h * 128:(h + 1) * 128], start=False, stop=False, skip_group_check=True),
                             reads=[('qiT', d2), sk_], writes=[pk(4)])
                P.op('pe', lambda e, h=h, ob_=ob_: e.matmul(out=ob_, lhsT=scm[:, 1, h, :], rhs=vbA[:, h * 128:(h + 1) * 128], start=False, stop=True),
                     reads=[('scm', 1), vbk], writes=[pk(4)])
            yield
            for h in range(4):
                P.op('act', lambda e, h=h, jj=jj: e.activation(out=oa[:, jj, h * 128:(h + 1) * 128], in_=pbank[4][:, h * 128:(h + 1) * 128], func=AF.Square, accum_out=rs4[:, h:h + 1]),
                     reads=[pk(4)], writes=[('oa', jj), ('rs4', h)])
            P.op('act', lambda e: e.activation(out=rs4[:, 4:8], in_=rs4[:, 0:4], func=AF.Ln, scale=1.0 / 128, bias=EPS_AP), reads=[('rs4', h) for h in range(4)] + ['eps'], writes=['rs4b'])
            P.op('act', lambda e: e.activation(out=rs4[:, 4:8], in_=rs4[:, 4:8], func=AF.Exp, scale=-0.5), reads=['rs4b'], writes=['rs4b'])
            P.op('dve', lambda e, jj=jj: e.tensor_tensor(out=oa[:, jj, :].rearrange("p (h e) -> p h e", h=4), in0=pbank[4].rearrange("p (h e) -> p h e", h=4),
                                                        in1=rs4[:, 4:8].unsqueeze(2).to_broadcast([128, 4, 128]), op=ALU.mult),
                 reads=[pk(4), 'rs4b'], writes=[('oa', jj)])
            yield
        sV = ring_next()
        wvV = v3(sV, 0, 8, 512)

        def bv_tile(jj):
            pbv = 0
            proj_tok(pbv, hTg[:, :, jj * 128:(jj + 1) * 128], ('hTg', jj), lambda k: wvV[:, k, :], ('ring', sV))
            gelu_from_psum(pbv, gu[:, jj, :], ('gu', jj), (0, 1))
            yield
            gv = gu[:, jj, :]
            gvk = ('gu', jj)
            P.op('dve', lambda e, gv=gv: e.bn_stats(out=bnst[:, 0:6], in_=gv), reads=[gvk], writes=['bnst'])
            P.op('dve', lambda e: e.bn_aggr(out=bnst[:, 6:8], in_=bnst[:, 0:6]), reads=['bnst'], writes=['bnst2'])
            P.op('act', lambda e: e.activation(out=ss[:, 12:13], in_=bnst[:, 7:8], func=AF.Ln, bias=EPS_AP, scale=1.0), reads=['bnst2', 'eps'], writes=['ss4'])
            P.op('act', lambda e: e.activation(out=ss[:, 13:14], in_=ss[:, 12:13], func=AF.Exp, scale=-0.5), reads=['ss4'], writes=['ss5'])
            P.op('dve', lambda e, gv=gv: e.tensor_scalar(out=gv, in0=gv, scalar1=bnst[:, 6:7], scalar2=ss[:, 13:14], op0=ALU.subtract, op1=ALU.mult),
                 reads=[gvk, 'bnst2', 'ss5'], writes=[gvk])
            P.op('pool', lambda e, gv=gv: e.tensor_tensor(out=gv, in0=gv, in1=lng, op=ALU.mult), reads=[gvk, 'lng'], writes=[gvk])
            yield
        drive([stageA_tile(jj) for jj in range(G)] + [bv_tile(jj) for jj in range(G)], A_SKEW)
        s = ring_next()
        wv = v3(s, 0, 8, 512)
        for jj in range(G):
            tsl = slice(jj * 128, (jj + 1) * 128)
            pb_ = 1 + jj % 2
            st_ = t512[jj % 2]
            stk = T5K[jj % 2]
            proj_tok(pb_, hTg[:, :, tsl], ('hTg', jj), lambda k: wv[:, k, :], ('ring', s))
            P.op('act', lambda e, pb_=pb_, st_=st_: e.activation(out=st_, in_=pbank[pb_], func=AF.Silu), reads=[pk(pb_)], writes=[stk])
            P.op('dve', lambda e, jj=jj, st_=st_: e.tensor_tensor(out=oag2[jj % 2], in0=oa[:, jj, :], in1=st_, op=ALU.mult), reads=[('oa', jj), stk], writes=[('oag', jj % 2)])
        for jj in range(G):
            tb_ = 3 + jj % 2
            for h in range(4):
                P.op('pe', lambda e, h=h, tb_=tb_, jj=jj: e.transpose(out=pbf[tb_][:, h * 128:(h + 1) * 128], in_=oag2[jj % 2][:, h * 128:(h + 1) * 128], identity=identb),
                     reads=[('oag', jj % 2), 'identb'], writes=[pk(tb_)])
        for jj in range(G):
            tsl = slice(jj * 128, (jj + 1) * 128)
            tb_ = 3 + jj % 2
            P.op('act', lambda e, tsl=tsl, tb_=tb_: e.activation(out=oaT[:, :, tsl], in_=pbf[tb_][:, 0:512].rearrange("p (h t) -> p h t", h=4), func=AF.Identity,
                                                        scale=vc[:, V_GNA:V_GNA + 1], bias=0.0), reads=[pk(tb_), 'vc'], writes=[('oaT', jj)])
        s = ring_next()
        wv = v3(s, 0, 8, 512)
        for jj in range(G):
            tsl = slice(jj * 128, (jj + 1) * 128)
            pb_ = 5 + jj % 2
            proj_tok(pb_, hTg[:, :, tsl], ('hTg', jj), lambda k: wv[:, k, :], ('ring', s))
            gelu_from_psum(pb_, t512[2 + jj % 2], T5K[2 + jj % 2], (0, 1))
        hkeys = [kx for jj in range(G) for kx in HK(('hTg', jj))]
        for jj in range(G):
            mb_ = [7, 6][jj % 2]
            gv = gu[:, jj, :]
            gvk = ('gu', jj)
            ug = t512[2 + jj % 2]
            ugk = T5K[2 + jj % 2]
            vnb_ = vnb2[jj % 2]
            obm_ = obm2[jj % 2]
            P.op('pool', lambda e, gv=gv, vnb_=vnb_: e.tensor_tensor(out=vnb_, in0=gv, in1=lnb, op=ALU.add), reads=[gvk, 'lnb'], writes=[('vnb', jj % 2)])
            for g4 in range(4):
                P.op('pe', lambda e, g4=g4, mb_=mb_, vnb_=vnb_: e.matmul(out=pbank[mb_][:, g4 * 128:(g4 + 1) * 128], lhsT=wsT[:, g4, :], rhs=vnb_[:, g4 * 128:(g4 + 1) * 128],
                                                                   start=True, stop=True),
                     reads=['wsT', ('vnb', jj % 2)], writes=[pk(mb_)])
            for g4 in range(4):
                P.op('dve', lambda e, g4=g4, ug=ug, mb_=mb_, obm_=obm_: e.scalar_tensor_tensor(out=obm_[:, g4 * 128:(g4 + 1) * 128], in0=pbank[mb_][:, g4 * 128:(g4 + 1) * 128],
                                                                          scalar=vc[:, V_BS + g4:V_BS + g4 + 1], in1=ug[:, g4 * 128:(g4 + 1) * 128], op0=ALU.add, op1=ALU.mult),
                     reads=[pk(mb_), 'vc', ugk], writes=[('obm', jj % 2, g4)])

        def c_gates(jb):
            s = ring_next()
            wga = v3(s, 0, 8, 128)
            wgb = v3(s, 1024, 8, 128)
            rk = ('ring', s)
            o_ = 4 * (jb % 2)
            for k in range(8):
                P.op('pe', lambda e, k=k: e.matmul(out=pbank[o_ + 0][:, 0:T], lhsT=wga[:, k, :], rhs=hTg[:, k, :], start=(k == 0), stop=(k == 7)), reads=hkeys + [rk], writes=[pk(o_ + 0)])
            for k in range(8):
                P.op('pe', lambda e, k=k: e.matmul(out=pbank[o_ + 1][:, 0:T], lhsT=wgb[:, k, :], rhs=hTg[:, k, :], start=(k == 0), stop=(k == 7)), reads=hkeys + [rk], writes=[pk(o_ + 1)])
            return s

        def c_rest(jb, s):
            wpa_ = v3(s, 2048, 4, 128)
            wpb_ = v3(s, 2560, 4, 128)
            rk = ('ring', s)
            o_ = 4 * (jb % 2)
            gts = gtsets[jb % 2]
            gks = gtkeys[jb % 2]
            for k in range(4):
                P.op('pe', lambda e, k=k: e.matmul(out=pbank[o_ + 2][:, 0:T], lhsT=wpa_[:, k, :], rhs=oaT[:, k, :], start=(k == 0), stop=(k == 3)),
                     reads=[('oaT', jj) for jj in range(G)] + [rk], writes=[pk(o_ + 2)])
            for k in range(4):
                P.op('pe', lambda e, k=k: e.matmul(out=pbank[o_ + 3][:, 0:T], lhsT=wpb_[:, k, :], rhs=obT[:, k, :], start=(k == 0), stop=(k == 3)),
                     reads=[('obT', jj) for jj in range(G)] + [rk], writes=[pk(o_ + 3)])
            P.op('act', lambda e: e.activation(out=gts[0], in_=pbank[o_ + 0][:, 0:T], func=AF.Sigmoid), reads=[pk(o_ + 0)], writes=[gks[0]])
            P.op('act', lambda e: e.activation(out=gts[1], in_=pbank[o_ + 1][:, 0:T], func=AF.Sigmoid), reads=[pk(o_ + 1)], writes=[gks[1]])
            P.op('dve', lambda e: e.tensor_tensor(out=gts[2], in0=pbank[o_ + 2][:, 0:T], in1=gts[0], op=ALU.mult), reads=[pk(o_ + 2), gks[0]], writes=[gks[2]])
            P.op('dve', lambda e: e.tensor_tensor(out=gts[3], in0=pbank[o_ + 3][:, 0:T], in1=gts[1], op=ALU.mult), reads=[pk(o_ + 3), gks[1]], writes=[gks[3]])
            P.op('pool', lambda e: e.tensor_tensor(out=mT[:, jb, :], in0=gts[2], in1=gts[3], op=ALU.add), reads=[gks[2], gks[3]], writes=[('mT', jb)])

        s0 = c_gates(0)
        for jj in range(G):
            tsl = slice(jj * 128, (jj + 1) * 128)
            tb_ = 3 + jj % 2
            obm_ = obm2[jj % 2]
            for g4 in range(4):
                P.op('pe', lambda e, g4=g4, tb_=tb_, obm_=obm_: e.transpose(out=pbf[tb_][:, g4 * 128:(g4 + 1) * 128], in_=obm_[:, g4 * 128:(g4 + 1) * 128], identity=identb),
                     reads=[('obm', jj % 2, g4), 'identb'], writes=[pk(tb_)])
            P.op('act', lambda e, tsl=tsl, tb_=tb_: e.activation(out=obT[:, :, tsl], in_=pbf[tb_][:, 0:512].rearrange("p (h t) -> p h t", h=4), func=AF.Copy), reads=[pk(tb_)], writes=[('obT', jj)])
        c_rest(0, s0)
        for jb in range(1, 8):
            c_rest(jb, c_gates(jb))
        mkeys = [('mT', jb) for jb in range(8)]
        sD0 = ring_next()
        sD1 = ring_next(hold=1)
        wvs = [v3(sD0, 0, 8, 512), v3(sD1, 0, 8, 512)]
        wks = [('ring', sD0), ('ring', sD1)]

        def de_tile(jj):
            tsl = slice(jj * 128, (jj + 1) * 128)
            for hf in range(2):
                bk = 2 + hf + 2 * (jj % 2)
                ti_ = 2 + hf
                for k in range(8):
                    P.op('pe', lambda e, k=k, bk=bk, hf=hf: e.matmul(out=pbank[bk], lhsT=mT[:, k, tsl], rhs=wvs[hf][:, k, :], start=(k == 0), stop=(k == 7)),
                         reads=mkeys + [wks[hf]], writes=[pk(bk)])
                P.op('dve', lambda e, bk=bk, hf=hf, ti_=ti_: e.tensor_tensor(out=t512[ti_], in0=pbank[bk], in1=G2[:, hf * 512:(hf + 1) * 512], op=ALU.mult),
                     reads=[pk(bk), 'G2'], writes=[T5K[ti_]])
                P.op('pool', lambda e, hf=hf, ti_=ti_: e.tensor_tensor(out=xg[:, jj, hf * 512:(hf + 1) * 512], in0=xg[:, jj, hf * 512:(hf + 1) * 512], in1=t512[ti_], op=ALU.add),
                     reads=[xgk(jj), T5K[ti_]], writes=[xgk(jj)])
            yield
            yield from front(None, xg[:, jj, :], xgk(jj), lambda blk: A2[:, blk, b:b + 1], lambda blk: modc[:, 24 + blk, b:b + 1], hTg[:, :, tsl], ('hTg', jj),
                             load=False, slot=2 + jj % 2, bank=6 + jj % 2)
        drive([de_tile(jj) for jj in range(G)], 1)
        if g + 1 < NT // G:
            load_xg(b, g + 1)
        nfr = []
        if g + 1 < NT // G:
            xn_ = xgs[(g + 1) % 2]
            nfr = [front(None, xn_[:, jj, :], ('xg', (g + 1) % 2, jj), lambda blk: A1[:, blk, b:b + 1], lambda blk: modc[:, blk, b:b + 1],
                         hTg[:, :, jj * 128:(jj + 1) * 128], ('hTg', jj), load=False, slot=2 + jj % 2, bank=6 + jj % 2) for jj in range(G)]
        for jf in range(22):
            if jf == 12:
                for fr in nfr:
                    next(fr)
            if jf % 2 == 0:
                s = ring_next()
            wa = v3(s, (jf % 2) * 2048, 8, 128)
            wb_ = v3(s, (jf % 2) * 2048 + 1024, 8, 128)
            rk = ('ring', s)
            ba = 2 * (jf % 3)
            for k in range(8):
                P.op('pe', lambda e, k=k, ba=ba, wa=wa: e.matmul(out=pbank[ba][:, 0:T], lhsT=wa[:, k, :], rhs=hTg[:, k, :], start=(k == 0), stop=(k == 7)), reads=hkeys + [rk], writes=[pk(ba)])
            for k in range(8):
                P.op('pe', lambda e, k=k, ba=ba, wb_=wb_: e.matmul(out=pbank[ba + 1][:, 0:T], lhsT=wb_[:, k, :], rhs=hTg[:, k, :], start=(k == 0), stop=(k == 7)), reads=hkeys + [rk], writes=[pk(ba + 1)])
            gi = jf % 2
            P.op('act', lambda e, ba=ba, gi=gi: e.activation(out=gtF[gi], in_=pbank[ba][:, 0:T], func=AF.Silu), reads=[pk(ba)], writes=[('gtF', gi)])
            P.op('dve', lambda e, ba=ba, gi=gi, jf=jf: e.tensor_tensor(out=actT[:, jf, :], in0=pbank[ba + 1][:, 0:T], in1=gtF[gi], op=ALU.mult), reads=[pk(ba + 1), ('gtF', gi)], writes=[('actT', jf)])
        akeys = [('actT', jf) for jf in range(22)]
        if nfr:
            for fr in nfr:
                next(fr)
        for q4 in range(4):
            if q4 == 1:
                for fr in nfr:
                    for _ in fr:
                        pass
            s = ring_next()
            wv = v3(s, 0, 22, 256)
            for jj in range(G):
                tsl = slice(jj * 128, (jj + 1) * 128)
                bk = 4 + (jj % 2)
                for k in range(22):
                    P.op('pe', lambda e, k=k, tsl=tsl, bk=bk, wv=wv: e.matmul(out=pbank[bk][:, 0:256], lhsT=actT[:, k, tsl], rhs=wv[:, k, :], start=(k == 0), stop=(k == 21)),
                         reads=akeys + [('ring', s)], writes=[pk(bk)])
                P.op('dve', lambda e, bk=bk, q4=q4: e.tensor_tensor(out=t512[3][:, 0:256], in0=pbank[bk][:, 0:256], in1=G5[:, q4 * 256:(q4 + 1) * 256], op=ALU.mult),
                     reads=[pk(bk), 'G5'], writes=[T5K[3]])
                P.op('pool', lambda e, jj=jj, q4=q4: e.tensor_tensor(out=xg[:, jj, q4 * 256:(q4 + 1) * 256], in0=xg[:, jj, q4 * 256:(q4 + 1) * 256], in1=t512[3][:, 0:256], op=ALU.add),
                     reads=[xgk(jj), T5K[3]], writes=[xgk(jj)])
        for jj in range(G):
            j = g * G + jj
            pj = jj % 2
            c0 = 16 + 3 * pj
            kH = [('ssH', pj, i) for i in range(3)]
            P.op('act', lambda e, jj=jj, pj=pj, c0=c0: e.activation(out=xn2[pj], in_=xg[:, jj, :], func=AF.Square, accum_out=ss[:, c0:c0 + 1]),
                 reads=[xgk(jj)], writes=[kH[0], ('xn', pj)])
            P.op('act', lambda e, c0=c0: e.activation(out=ss[:, c0 + 1:c0 + 2], in_=ss[:, c0:c0 + 1], func=AF.Ln, scale=1.0 / D, bias=EPS_AP), reads=[kH[0], 'eps'], writes=[kH[1]])
            P.op('act', lambda e, c0=c0: e.activation(out=ss[:, c0 + 2:c0 + 3], in_=ss[:, c0 + 1:c0 + 2], func=AF.Exp, scale=-0.5), reads=[kH[1]], writes=[kH[2]])
            P.op('dve', lambda e, jj=jj, c0=c0: e.scalar_tensor_tensor(out=xg[:, jj, :], in0=xg[:, jj, :], scalar=ss[:, c0 + 2:c0 + 3], in1=gfin, op0=ALU.mult, op1=ALU.mult),
                 reads=[xgk(jj), kH[2], 'gfin'], writes=[xgk(jj)])
            P.dma('sp', out[b * L + j * 128: b * L + (j + 1) * 128, :], xg[:, jj, :], reads=[xgk(jj)], sem=('yo', pj))

    P.barrier()
    for b in range(NB):
        bcast_tile(G2, lambda b=b: modc[:, 16:24, b], ['modc'], 'G2')
        bcast_tile(G5, lambda b=b: modc[:, 40:48, b], ['modc'], 'G5')
        P.op('pool', lambda e: e.memset(Srun.rearrange("p a b -> p (a b)"), 0.0), writes=[('Srun', d_, h_) for d_ in range(2) for h_ in range(4)])
        seqf = [('c', i) for i in range(NCT)] + [('l', i) for i in range(NT)]
        seqb = [('c', i) for i in reversed(range(NCT))] + [('l', i) for i in reversed(range(NT))]
        load_xg(b, 0)
        gens = []
        for step in range(NCT + NT):
            gens.append(p1_tile(b, 0, seqf[step][0], seqf[step][1], step))
            gens.append(p1_tile(b, 1, seqb[step][0], seqb[step][1], step))
        if b == 0:
            drive_first([late_precast_a(), late_precast_b()], gens, P1_SKEW)
        else:
            drive(gens, P1_SKEW, per_start=2)
        for g in range(NT // G):
            p2_group(b, g, fronts_done=(g > 0))
    P.finish()
    P.replay()
    return nc


_CACHE = {}


def _core_inputs(inp, b0, NB, L):
    f = lambda a: np.ascontiguousarray(np.asarray(a, dtype=np.float32))
    x = f(inp['x'])[b0:b0 + NB].reshape(NB * L, D)
    ctx = f(inp['ctx'])[b0:b0 + NB].reshape(NB * CTX, D)
    cvec = np.concatenate([f(inp['c'])[b0:b0 + NB], f(inp['c_ctx'])[None, :]], 0)
    if NB < 4:
        cvec = np.concatenate([cvec[:NB], np.zeros((4 - NB, D), np.float32), cvec[NB:]], 0)
    vrows = np.concatenate([
        f(inp['b_mod'])[0].reshape(48, 128), f(inp['g_mix'])[0].reshape(8, 128), f(inp['g_ffn'])[0].reshape(8, 128),
        f(inp['lb_gamma']).reshape(16, 128), f(inp['g_norm_a'])[0].reshape(1, 128), f(inp['b_s'])[0].reshape(4, 128)], 0)
    return {
        'x': x, 'ctx': ctx, 'cvec': np.ascontiguousarray(cvec), 'w_mod': f(inp['w_mod'])[0], 'vrows': np.ascontiguousarray(vrows),
        'w_in': f(inp['w_in'])[0], 'ln_v_g': f(inp['ln_v_g'])[0], 'ln_v_b': f(inp['ln_v_b'])[0], 'w_s': f(inp['w_s'])[0],
        'w_pa': f(inp['w_pa'])[0], 'w_pb': f(inp['w_pb'])[0], 'w_o': f(inp['w_o'])[0], 'w_up': f(inp['w_up'])[0],
        'w_down': f(inp['w_down'])[0], 'g_final': f(inp['g_final']), 'cst': make_consts(),
    }


def run(inp, ncores, G=2, use_gelu_tanh=True):
    import os
    use_gelu_tanh = use_gelu_tanh and os.environ.get('GT', '1') == '1'
    B, L = inp['x'].shape[0], inp['x'].shape[1]
    NB = B // ncores
    NT = L // 128
    key = (NB, NT, G, use_gelu_tanh)
    import os
    nc = build(NB, NT, G, use_gelu_tanh, P1_SKEW=int(os.environ.get('P1S', '3')), A_SKEW=int(os.environ.get('AS', '4')))
    in_maps = [_core_inputs(inp, i * NB, NB, L) for i in range(ncores)]
    res = run_bass_kernel_spmd(nc, in_maps, core_ids=list(range(ncores)))
    outs = [np.asarray(r['out']).reshape(NB, L, D) for r in res.results]
    return np.concatenate(outs, 0).astype(np.float32)


def kernel(**inputs):
    return run(inputs, NCORES, G=2)
```

```python
import numpy as np
import concourse.bass as bass
import concourse.mybir as mybir
from concourse.bass_utils import run_bass_kernel_spmd

F32 = mybir.dt.float32
BF16 = mybir.dt.bfloat16
AF = mybir.ActivationFunctionType
ALU = mybir.AluOpType

D = 1024
CTX = 256
DFF = 2816
INC = 5632
EPS = 1e-6
NCORES = 8


class Prog:
    ENG = ['pe', 'act', 'dve', 'pool', 'sp']

    def __init__(s, nc):
        s.nc = nc
        s.q = {e: [] for e in s.ENG}
        s.cnt = {}
        s.waited = {e: {} for e in s.ENG}
        s.lastw = {}
        s.rd = {}
        s.sems = {}
        s.owner = None
        s.wown = {}
        s.xown_ok = set()

    def _sem(s, sk):
        if sk not in s.sems:
            s.sems[sk] = s.nc.alloc_semaphore("s_" + "_".join(str(x) for x in sk))
            s.cnt[sk] = 0
        return s.sems[sk]

    def _emit(s, eng, fn, reads, writes, sk, inc):
        deps = {}
        if s.owner is not None:
            for r in reads:
                o = s.wown.get(r)
                base = r
                while isinstance(base, tuple):
                    base = base[0]
                if o is not None and o != s.owner and base not in s.xown_ok:
                    raise RuntimeError("pipeline clobber: key %r written by %r read by %r" % (r, o, s.owner))
        for w in writes:
            s.wown[w] = s.owner

        def add(ev):
            if ev is None:
                return
            k, v = ev
            if deps.get(k, 0) < v:
                deps[k] = v
        for r in reads:
            add(s.lastw.get(r))
        for w in writes:
            add(s.lastw.get(w))
            for k, v in s.rd.get(w, {}).items():
                add((k, v))
        waits = []
        for k, v in deps.items():
            if k[0] == 'd':
                v = s.cnt[k]
            if k == ('e', 'pe') and eng == 'pe':
                continue
            if s.waited[eng].get(k, 0) >= v:
                continue
            s.waited[eng][k] = v
            waits.append((k, v))
        s._sem(sk)
        s.cnt[sk] += inc
        ev = (sk, s.cnt[sk])
        s.q[eng].append((waits, fn, sk, inc))
        for r in reads:
            d = s.rd.setdefault(r, {})
            if d.get(sk, 0) < ev[1]:
                d[sk] = ev[1]
        for w in writes:
            s.lastw[w] = ev
            s.rd[w] = {}
        return ev

    def op(s, eng, fn, reads=(), writes=()):
        return s._emit(eng, fn, reads, writes, ('e', eng), 1)

    def dma(s, q, out, in_, reads=(), writes=(), sem=None, **kw):
        return s._emit(q, lambda e: e.dma_start(out=out, in_=in_, **kw), reads, writes, ('d', sem), 16)

    def barrier(s):
        for eng in s.ENG:
            waits = [(k, v) for k, v in s.cnt.items() if v > 0 and s.waited[eng].get(k, 0) < v]
            for k, v in waits:
                s.waited[eng][k] = v
            s.q[eng].append((waits, None, None, 0))

    def finish(s, eng='sp'):
        waits = [(k, v) for k, v in s.cnt.items() if v > 0 and s.waited[eng].get(k, 0) < v]
        s.q[eng].append((waits, None, None, 0))

    def replay(s):
        nc = s.nc

        def run(name):
            def f(e):
                for waits, fn, sk, inc in s.q[name]:
                    for k, v in waits:
                        e.wait_ge(s.sems[k], v)
                    if fn is not None:
                        fn(e).then_inc(s.sems[sk], inc)
            return f
        with nc.Block() as block:
            block.tensor(run('pe'))
            block.scalar(run('act'))
            block.vector(run('dve'))
            block.gpsimd(run('pool'))
            block.sync(run('sp'))


C_ID = 0
C_AF = 128
C_AB = 256
C_MF = 384
C_MB = 512
C_CH = 640
C_ONE = 642
C_N = 770


def make_consts():
    c = np.zeros((128, C_N), np.float32)
    i = np.arange(128)
    same = (i[:, None] // 64) == (i[None, :] // 64)
    le = (i[:, None] <= i[None, :]) & same
    ge = (i[:, None] >= i[None, :]) & same
    c[:, C_ID:C_ID + 128] = np.eye(128)
    c[:, C_AF:C_AF + 128] = le.astype(np.float32) - 0.5 * same
    c[:, C_AB:C_AB + 128] = ge.astype(np.float32) - 0.5 * same
    c[:, C_MF:C_MF + 128] = le
    c[:, C_MB:C_MB + 128] = ge
    c[:, C_CH] = (i < 64)
    c[:, C_CH + 1] = (i >= 64)
    c[:, C_ONE:C_ONE + 128] = 1.0
    return c


V_BMOD = 0
V_GMIX = 48
V_GFFN = 56
V_LBG = 64
V_GNA = 80
V_BS = 81
V_N = 85


def build(NB, NT, G, use_gelu_tanh=True, P1_SKEW=3, A_SKEW=4):
    import os
    EVS = os.environ.get('EVS', '0') == '1'
    L = NT * 128
    NCT = CTX // 128
    NCH = NT * 2
    nc = bass.Bass("TRN2", target_bir_lowering=False, dynamic_dma_scratch_size=512)
    P = Prog(nc)
    P.xown_ok.update(['Srun', 'sst_d', 'hTg'])

    def din(name, shape, dt=F32):
        return nc.dram_tensor(name, list(shape), dt, kind="ExternalInput").ap()
    x = din("x", [NB * L, D])
    ctx = din("ctx", [NB * CTX, D])
    cvec = din("cvec", [5, D])
    w_mod = din("w_mod", [D, 6 * D])
    vrows = din("vrows", [V_N, 128])
    w_in = din("w_in", [D, INC])
    lnvg = din("ln_v_g", [512])
    lnvb = din("ln_v_b", [512])
    w_s = din("w_s", [4, 128, 128])
    w_pa = din("w_pa", [512, D])
    w_pb = din("w_pb", [512, D])
    w_o = din("w_o", [D, D])
    w_up = din("w_up", [D, INC])
    w_down = din("w_down", [DFF, D])
    g_final = din("g_final", [D])
    cst_d = din("cst", [128, C_N])
    out = nc.dram_tensor("out", [NB * L, D], F32, kind="ExternalOutput").ap()

    def dscr(name, shape, dt=BF16):
        return nc.dram_tensor(name, list(shape), dt, kind="Internal").ap()
    sH = dscr("sH", [128, 8 * 2048])
    sB = dscr("sB", [3, 128, 8 * 512])
    sC = dscr("sC", [8, 128, 3072])
    sD = dscr("sD", [2, 128, 8 * 512])
    sF = dscr("sF", [11, 128, 4096])
    sG = dscr("sG", [4, 128, 22 * 256])
    sst_d = dscr("sst_d", [NB, 2, NCH, 128, 512])

    def sb(name, shape, dt=F32):
        return nc.alloc_sbuf_tensor("sb_" + name, list(shape), dt).ap()

    def pst(name, shape, dt=F32):
        return nc.alloc_psum_tensor("ps_" + name, list(shape), dt).ap()

    T = G * 128
    cst = sb("cst", [128, C_N])
    identf = cst[:, C_ID:C_ID + 128]
    onesf = cst[:, C_ONE:C_ONE + 128]
    identb = sb("identb", [128, 128], BF16)
    maskb = sb("maskb", [128, 2, 128], BF16)
    vc = sb("vc", [128, V_N])
    modc = sb("modc", [128, 48, 5])
    A1 = sb("A1", [128, 8, 5])
    A2 = sb("A2", [128, 8, 5])
    omlc = sb("omlc", [128, 8])
    OML = sb("OML", [128, 1024])
    G2 = sb("G2", [128, 1024])
    G5 = sb("G5", [128, 1024])
    gfin = sb("gfin", [128, 1024])
    lng = sb("lng", [128, 512])
    lnb = sb("lnb", [128, 512])
    wsT = sb("wsT", [128, 4, 128], BF16)
    scT = sb("scT", [128, 8, 5])
    win_h = sb("win_h", [128, 8, 2048], BF16)
    NSLOT = 3
    SLOTN = 5632
    ring = [sb("ring%d" % i, [128, SLOTN], BF16) for i in range(NSLOT)]
    stgf = [sb("stgf%d" % i, [128, 2048]) for i in range(2)]
    Rt = stgf[0][:, 1024:2048].rearrange("p (a b) -> p a b", a=8)
    Srun = sb("Srun", [128, 2, 512])
    xgs = [sb("xg%d" % i, [128, G, 1024]) for i in range(2)]
    hTg = sb("hTg", [128, 8, T], BF16)
    xt1 = [stgf[1][:, i * 1024:(i + 1) * 1024] for i in range(2)]
    hT1 = [sb("hT1_%d" % i, [128, 8, 128], BF16) for i in range(2)]
    xn2 = [sb("xn%d" % i, [128, 1024], BF16) for i in range(2)]
    ss = sb("ss", [128, 24])
    sf = sb("sf", [128, 1024])
    kf = sb("kf", [128, 1024])
    lf = sb("lf", [128, 1024])
    Ep = sb("Ep", [128, 1024])
    Em = sb("Em", [128, 1024])
    t512 = [sf[:, 0:512], sf[:, 512:1024], Ep[:, 0:512], Ep[:, 512:1024]]
    T5K = {0: ('sf', 0), 1: ('sf', 1), 2: ('Ep', 0), 3: ('Ep', 1)}
    assert T <= 256
    gt = [Em[:, i * 256:i * 256 + T] for i in range(4)]
    GTK = {i: ('Em', i // 2) for i in range(4)}
    gtsets = [gt, [kf[:, i * 256:i * 256 + T] for i in range(4)]]
    gtkeys = [GTK, {i: ('kf', i // 2) for i in range(4)}]
    gtF = [sb("gtF%d" % i, [128, T]) for i in range(2)]
    kiT = sb("kiT", [128, 8, 128], BF16)
    qiT = sb("qiT", [128, 8, 128], BF16)
    kib2 = [sb("kib%d" % i, [128, 512], BF16) for i in range(2)]
    vb2 = [sb("vb%d" % i, [128, 512], BF16) for i in range(2)]
    vbA2 = [sb("vbA%d" % i, [128, 512], BF16) for i in range(2)]
    expm2 = [sb("expm%d" % i, [128, 4, 2]) for i in range(2)]
    decay2 = [sb("decay%d" % i, [128, 4, 2]) for i in range(2)]
    tmpS2 = [sb("tmpS%d" % i, [128, 512]) for i in range(2)]
    sstw = [sb("sstw%d" % i, [128, 512], BF16) for i in range(4)]
    sstr2 = [sb("sstr%d" % i, [128, 2, 2, 512], BF16) for i in range(2)]
    scm = sb("scm", [128, 2, 4, 128], BF16)
    oa = sb("oa", [128, G, 512])
    rs4 = sb("rs4", [128, 8])
    oag2 = [sb("oag%d" % i, [128, 512], BF16) for i in range(2)]
    gu = sb("gu", [128, G, 512])
    vnb2 = [sb("vnb%d" % i, [128, 512], BF16) for i in range(2)]
    obm2 = [sb("obm%d" % i, [128, 512], BF16) for i in range(2)]
    bnst = sb("bnst", [128, 8])
    oaT = sb("oaT", [128, 4, T], BF16)
    obT = sb("obT", [128, 4, T], BF16)
    mT = sb("mT", [128, 8, T], BF16)
    actT = sb("actT", [128, 22, T], BF16)
    yt = [stgf[0][:, 0:1024]] * 2
    pbank = [pst("pb%d" % i, [128, 512]) for i in range(8)]
    pbf = [pbank[i].bitcast(BF16) for i in range(8)]

    pk = lambda i: ('ps', i)

    P.dma('sp', cst, cst_d, writes=['cst'], sem='cst')
    P.op('dve', lambda e: e.tensor_copy(out=identb, in_=identf), reads=['cst'], writes=['identb'])
    P.op('dve', lambda e: e.tensor_copy(out=maskb[:, 0, :], in_=cst[:, C_MF:C_MF + 128]), reads=['cst'], writes=['maskb'])
    P.op('dve', lambda e: e.tensor_copy(out=maskb[:, 1, :], in_=cst[:, C_MB:C_MB + 128]), reads=['cst'], writes=['maskb'])
    P.dma('sp', gfin, g_final.partition_broadcast(128), writes=['gfin'], sem='c1')
    P.dma('sp', lng, lnvg.partition_broadcast(128), writes=['lng'], sem='c1')
    P.dma('sp', lnb, lnvb.partition_broadcast(128), writes=['lnb'], sem='c1')

    cast_i = [0]

    def precast(wsrc, KC, colw, dests, late=False, sset=0, c_lo=0, c_hi=None):
        ncols = wsrc.shape[1] if c_hi is None else c_hi
        src = wsrc.rearrange("(k p) c -> p k c", p=128)
        for c0 in range(c_lo, ncols, colw):
            cast_i[0] += 1
            if late:
                i = sset
                if i == 0:
                    fbuf = actT.rearrange("p a b -> p (a b)").bitcast(F32)
                    bbuf = mT.rearrange("p a b -> p (a b)")
                    fk = [('actT', jf) for jf in range(22)]
                    bk_ = [('mT', jb) for jb in range(8)]
                else:
                    fbuf = xgs[1].rearrange("p a b -> p (a b)")
                    bbuf = hTg.rearrange("p a b -> p (a b)")
                    fk = [('xg', 1, jj) for jj in range(G)]
                    bk_ = [(('hTg', jj), blk) for jj in range(G) for blk in range(8)]
                sfv = fbuf[:, 0:KC * colw].rearrange("p (k c) -> p k c", k=KC)
                sbv = bbuf[:, 0:KC * colw].rearrange("p (k c) -> p k c", k=KC)
                eng = ['dve', 'pool'][cast_i[0] % 2]
                semi, semo = ('lstg_i', i), ('lstg_o', i)
            else:
                i = cast_i[0] % 2
                sfv = stgf[i][:, 0:KC * colw].rearrange("p (k c) -> p k c", k=KC)
                sbv = ring[i][:, 0:KC * colw].rearrange("p (k c) -> p k c", k=KC)
                fk = [('stgf', i)]
                bk_ = [('ring', i)]
                eng = ['dve', 'pool', 'act'][cast_i[0] % 3]
                semi, semo = ('stgf', i), ('cst_o', i)
            P.dma('sp', sfv, src[:, :, c0:c0 + colw], writes=fk, sem=semi)
            if late:
                yield
            if eng == 'act':
                P.op('act', lambda e, a=sbv, b=sfv: e.activation(out=a, in_=b, func=AF.Copy), reads=fk, writes=bk_)
            else:
                P.op(eng, lambda e, a=sbv, b=sfv: e.tensor_copy(out=a, in_=b), reads=fk, writes=bk_)
            if late:
                yield
            for dst, off, w, key in dests(c0):
                P.dma('sp', dst, sbv[:, :, off:off + w], reads=bk_, writes=[key], sem=semo)
            yield

    def k3(ap2d, k):
        return ap2d.rearrange("p (k c) -> p k c", k=k)

    def d_win(c0):
        if c0 < 2048:
            return [(k3(sH, 8)[:, :, c0:c0 + 256], 0, 256, 'sH')]
        if c0 < 3584:
            p_, o_ = (c0 - 2048) // 512, (c0 - 2048) % 512
            return [(k3(sB[p_], 8)[:, :, o_:o_ + 256], 0, 256, ('sB', p_))]
        r = []
        for hb in range(2):
            c = c0 + hb * 128
            if c < 4608:
                j = (c - 3584) // 128
                r.append((k3(sC[j][:, 0:1024], 8), hb * 128, 128, ('sC', j)))
            else:
                j = (c - 4608) // 128
                r.append((k3(sC[j][:, 1024:2048], 8), hb * 128, 128, ('sC', j)))
        return r

    def d_wp(base):
        def f(c0):
            return [(k3(sC[c0 // 128 + q][:, base:base + 512], 4), q * 128, 128, ('sC', c0 // 128 + q)) for q in range(4)]
        return f

    def d_wo(c0):
        hf, o_ = c0 // 512, c0 % 512
        return [(k3(sD[hf], 8)[:, :, o_:o_ + 256], 0, 256, ('sD', hf))]

    def d_wup(c0):
        r = []
        for hb in range(2):
            c = c0 + hb * 128
            if c < DFF:
                j = c // 128
                o2 = (j % 2) * 2048
                r.append((k3(sF[j // 2][:, o2:o2 + 1024], 8), hb * 128, 128, ('sF', j // 2)))
            else:
                j = (c - DFF) // 128
                o2 = (j % 2) * 2048 + 1024
                r.append((k3(sF[j // 2][:, o2:o2 + 1024], 8), hb * 128, 128, ('sF', j // 2)))
        return r

    def d_wdn(c0):
        q4, o_ = c0 // 256, c0 % 256
        return [(k3(sG[q4], 22)[:, :, o_:o_ + 64], 0, 64, ('sG', q4))]

    for _ in precast(w_in, 8, 256, d_win):
        pass

    def late_precast_a():
        yield from precast(w_pa, 4, 512, d_wp(2048), late=True, sset=0)
        yield from precast(w_pb, 4, 512, d_wp(2560), late=True, sset=0)
        yield from precast(w_o, 8, 256, d_wo, late=True, sset=0)
        yield from precast(w_up, 8, 256, d_wup, late=True, sset=0, c_lo=0, c_hi=3840)

    def late_precast_b():
        yield from precast(w_up, 8, 256, d_wup, late=True, sset=1, c_lo=3840)
        yield from precast(w_down, 22, 64, d_wdn, late=True, sset=1)

    for q in range(4):
        P.dma('sp', win_h.rearrange("p k c -> p (k c)")[:, q * 4096:(q + 1) * 4096], sH[:, q * 4096:(q + 1) * 4096], reads=['sH'], writes=['win_h'], sem='win_h')

    c5 = sf[0:5, :]
    P.dma('sp', c5, cvec, writes=['sf'], sem='c5')
    P.op('act', lambda e: e.activation(out=c5, in_=c5, func=AF.Silu), reads=['sf'], writes=['sf'])
    for k in range(8):
        P.op('pe', lambda e, k=k: e.transpose(out=pbank[0][:, k * 5:(k + 1) * 5], in_=sf[0:5, k * 128:(k + 1) * 128], identity=identf[0:5, 0:5]),
             reads=['sf', 'cst'], writes=[pk(0)])
    P.op('dve', lambda e: e.tensor_copy(out=scT.rearrange("p k b -> p (k b)"), in_=pbank[0][:, 0:40]), reads=[pk(0)], writes=['scT'])
    vr = kf[0:V_N, 0:128]
    P.dma('sp', vr, vrows, writes=['kf'], sem='vr')
    P.op('pe', lambda e: e.transpose(out=pbank[1][:, 0:V_N], in_=vr, identity=identf[0:V_N, 0:V_N]), reads=['kf', 'cst'], writes=[pk(1)])
    P.op('dve', lambda e: e.tensor_copy(out=vc, in_=pbank[1][:, 0:V_N]), reads=[pk(1)], writes=['vc'])
    wmv = w_mod.rearrange("(k p) c -> p k c", p=128)
    for cg in range(24):
        i = cg % 2
        wv = stgf[i].rearrange("p (k c) -> p k c", k=8)
        P.dma('sp', wv, wmv[:, :, cg * 256:(cg + 1) * 256], writes=[('stgf', i)], sem=('stgf', i))
        for bl in range(2):
            blk = cg * 2 + bl
            for k in range(8):
                P.op('pe', lambda e, wv=wv, k=k, bl=bl, blk=blk: e.matmul(out=pbank[2][:, blk * 5:(blk + 1) * 5], lhsT=wv[:, k, bl * 128:(bl + 1) * 128],
                                                                       rhs=scT[:, k, :], start=(k == 0), stop=(k == 7)),
                     reads=[('stgf', i), 'scT'], writes=[pk(2)])
    P.op('dve', lambda e: e.tensor_tensor(out=modc, in0=pbank[2][:, 0:240].rearrange("p (a b) -> p a b", b=5),
                                          in1=vc[:, V_BMOD:V_BMOD + 48].unsqueeze(2).to_broadcast([128, 48, 5]), op=ALU.add),
         reads=[pk(2), 'vc'], writes=['modc'])
    P.op('dve', lambda e: e.scalar_tensor_tensor(out=A1, in0=modc[:, 8:16, :], scalar=1.0, in1=vc[:, V_GMIX:V_GMIX + 8].unsqueeze(2).to_broadcast([128, 8, 5]),
                                                 op0=ALU.add, op1=ALU.mult), reads=['modc', 'vc'], writes=['A1'])
    P.op('dve', lambda e: e.scalar_tensor_tensor(out=A2, in0=modc[:, 32:40, :], scalar=1.0, in1=vc[:, V_GFFN:V_GFFN + 8].unsqueeze(2).to_broadcast([128, 8, 5]),
                                                 op0=ALU.add, op1=ALU.mult), reads=['modc', 'vc'], writes=['A2'])
    P.op('dve', lambda e: e.tensor_tensor(out=omlc, in0=vc[:, V_LBG + 8:V_LBG + 16], in1=vc[:, V_LBG:V_LBG + 8], op=ALU.subtract), reads=['vc'], writes=['omlc'])
    P.op('act', lambda e: e.activation(out=omlc, in_=omlc, func=AF.Sigmoid), reads=['omlc'], writes=['omlc'])

    def bcast_tile(dst, colsrc_fn, rkeys, dkey):
        P.op('dve', lambda e: e.tensor_tensor(out=Rt, in0=identf.unsqueeze(1).to_broadcast([128, 8, 128]),
                                              in1=colsrc_fn().unsqueeze(2).to_broadcast([128, 8, 128]), op=ALU.mult),
             reads=['cst'] + rkeys, writes=['Rt'])
        for hf in range(2):
            P.op('pe', lambda e, hf=hf: e.matmul(out=pbank[3 + hf], lhsT=onesf, rhs=Rt[:, hf * 4:(hf + 1) * 4, :].rearrange("p a b -> p (a b)"), start=True, stop=True),
                 reads=['Rt', 'cst'], writes=[pk(3 + hf)])
            P.op('act', lambda e, hf=hf: e.activation(out=dst[:, hf * 512:(hf + 1) * 512], in_=pbank[3 + hf], func=AF.Copy), reads=[pk(3 + hf)], writes=[dkey])

    P.barrier()
    bcast_tile(OML, lambda: omlc, ['omlc'], 'OML')
    wsf = lf[:, 0:512].rearrange("p (g s) -> p g s", g=4)
    P.dma('sp', wsf, w_s.rearrange("g t s -> t g s"), writes=['lf'], sem='wsf')
    for g4 in range(4):
        P.op('pe', lambda e, g4=g4: e.transpose(out=pbank[5][:, g4 * 128:(g4 + 1) * 128], in_=wsf[:, g4, :], identity=identf), reads=['lf', 'cst'], writes=[pk(5)])
    P.op('dve', lambda e: e.tensor_copy(out=wsT.rearrange("p g t -> p (g t)"), in_=pbank[5]), reads=[pk(5)], writes=['wsT'])

    def front(src_ap, xbuf, xkey, Acols, Bcols, hT_out, hkey, load=True, sem=None, slot=0, bank=0):
        c0 = slot * 3
        k0, k1, k2 = ('ss', slot, 0), ('ss', slot, 1), ('ss', slot, 2)
        xn_ = xn2[slot % 2]
        xnk = ('xn', slot % 2)
        if load:
            P.dma('sp', xbuf, src_ap, writes=[xkey], sem=sem)
        P.op('act', lambda e: e.activation(out=xn_, in_=xbuf, func=AF.Square, accum_out=ss[:, c0:c0 + 1]), reads=[xkey], writes=[k0, xnk])
        P.op('act', lambda e: e.activation(out=ss[:, c0 + 1:c0 + 2], in_=ss[:, c0:c0 + 1], func=AF.Ln, scale=1.0 / D, bias=EPS_AP),
             reads=[k0, 'eps'], writes=[k1])
        P.op('act', lambda e: e.activation(out=ss[:, c0 + 2:c0 + 3], in_=ss[:, c0 + 1:c0 + 2], func=AF.Exp, scale=-0.5), reads=[k1], writes=[k2])
        P.op('dve', lambda e: e.tensor_scalar(out=xn_, in0=xbuf, scalar1=ss[:, c0 + 2:c0 + 3], scalar2=None, op0=ALU.mult), reads=[xkey, k2], writes=[xnk])
        yield
        for blk in range(8):
            P.op('pe', lambda e, blk=blk: e.transpose(out=pbf[bank][:, blk * 128:(blk + 1) * 128], in_=xn_[:, blk * 128:(blk + 1) * 128], identity=identb),
                 reads=[xnk, 'identb'], writes=[pk(bank)])
        yield
        for blk in range(8):
            if blk % 2 == 0 or not EVS:
                P.op('act', lambda e, blk=blk: e.activation(out=hT_out[:, blk, :], in_=pbf[bank][:, blk * 128:(blk + 1) * 128], func=AF.Identity,
                                                            scale=Acols(blk), bias=Bcols(blk)),
                     reads=[pk(bank), 'A1', 'A2', 'modc'], writes=[(hkey, blk)])
            else:
                P.op('dve', lambda e, blk=blk: e.tensor_scalar(out=hT_out[:, blk, :], in0=pbf[bank][:, blk * 128:(blk + 1) * 128], scalar1=Acols(blk), scalar2=Bcols(blk),
                                                               op0=ALU.mult, op1=ALU.add),
                     reads=[pk(bank), 'A1', 'A2', 'modc'], writes=[(hkey, blk)])
        yield

    def drive_first(bgs, gens, skew):
        bgs = list(bgs)
        pending = list(gens)
        active = []
        tick = 0
        while pending or active or bgs:
            if pending and tick % skew == 0:
                for _ in range(2):
                    if pending:
                        active.append(pending.pop(0))
            for bg in list(bgs):
                try:
                    P.owner = None
                    next(bg)
                except StopIteration:
                    bgs.remove(bg)
            for g_ in list(active):
                try:
                    P.owner = id(g_)
                    next(g_)
                except StopIteration:
                    active.remove(g_)
                P.owner = None
            tick += 1

    def HK(hkey):
        return [(hkey, blk) for blk in range(8)]

    def drive(gens, skew, per_start=1):
        pending = list(gens)
        active = []
        tick = 0
        while pending or active:
            if pending and tick % skew == 0:
                for _ in range(per_start):
                    if pending:
                        active.append(pending.pop(0))
            for g_ in list(active):
                try:
                    P.owner = id(g_)
                    next(g_)
                except StopIteration:
                    active.remove(g_)
                P.owner = None
            tick += 1

    epsT = sb("epsT", [128, 1])
    P.op('pool', lambda e: e.memset(epsT, EPS), writes=['eps'])
    EPS_AP = epsT[:, 0:1]

    def proj_tok(bank, hT, hkey, wcols, wkey, ncol=512):
        for k in range(8):
            P.op('pe', lambda e, k=k, r_=wcols(k): e.matmul(out=pbank[bank][:, 0:ncol], lhsT=hT[:, k, :], rhs=r_, start=(k == 0), stop=(k == 7)),
                 reads=HK(hkey) + [wkey], writes=[pk(bank)])

    def ffront(zbanks, ndir, dir0, half0=0, part=None):
        its = [(i, dir0 + i, half0 + i, slice((half0 + i) * 512, (half0 + i + 1) * 512)) for i in range(ndir)]
        for i, d, hh, sl in (its if part in (None, 0) else []):
            P.op('act', lambda e, i=i, sl=sl: e.activation(out=sf[:, sl], in_=pbank[zbanks[i]], func=AF.Exp), reads=[pk(zbanks[i])], writes=[('sf', hh)])
        for i, d, hh, sl in (its if part in (None, 0) else []):
            P.op('dve', lambda e, sl=sl: e.tensor_scalar(out=sf[:, sl], in0=sf[:, sl], scalar1=1.0, scalar2=None, op0=ALU.add), reads=[('sf', hh)], writes=[('sf', hh)])
            P.op('dve', lambda e, sl=sl: e.reciprocal(out=sf[:, sl], in_=sf[:, sl]), reads=[('sf', hh)], writes=[('sf', hh)])
            P.op('dve', lambda e, sl=sl, d=d: e.tensor_tensor(out=kf[:, sl], in0=sf[:, sl], in1=OML[:, d * 512:(d + 1) * 512], op=ALU.mult),
                 reads=[('sf', hh), 'OML'], writes=[('kf', hh)])
        for i, d, hh, sl in (its if part in (None, 1) else []):
            P.op('act', lambda e, sl=sl: e.activation(out=lf[:, sl], in_=kf[:, sl], func=AF.Ln, scale=-1.0, bias=ONE_AP), reads=[('kf', hh), 'eps'], writes=[('lf', hh)])

    oneT = sb("oneT", [128, 1])
    P.op('pool', lambda e: e.memset(oneT, 1.0), writes=['eps'])
    ONE_AP = oneT[:, 0:1]

    def p1_tile(b, dirn, kind, ti, step):
        i2 = dirn
        B0, B1, B2, B3 = 4 * dirn, 4 * dirn + 1, 4 * dirn + 2, 4 * dirn + 3
        hsl = slice(dirn * 512, (dirn + 1) * 512)
        kib_, vb_, expm_, decay_, tmpS_ = kib2[dirn], vb2[dirn], expm2[dirn], decay2[dirn], tmpS2[dirn]
        kk = lambda n: (n, dirn)
        if kind == 'c':
            src = ctx[b * CTX + ti * 128: b * CTX + (ti + 1) * 128, :]
            mi = 4
        else:
            src = x[b * L + ti * 128: b * L + (ti + 1) * 128, :]
            mi = b
        yield from front(src, xt1[i2], ('xt1', i2), lambda blk: A1[:, blk, mi:mi + 1], lambda blk: modc[:, blk, mi:mi + 1], hT1[i2], ('hT1', i2),
                         sem=('xt1', i2), slot=dirn, bank=B0)
        hk = ('hT1', i2)
        proj_tok(B0, hT1[i2], hk, lambda k: win_h[:, k, dirn * 512:(dirn + 1) * 512], 'win_h')
        proj_tok(B3, hT1[i2], hk, lambda k: win_h[:, k, 1024:1536], 'win_h')
        yield
        ffront([B0], 1, dirn, half0=dirn)
        P.op('dve', lambda e: e.tensor_copy(out=vb_, in_=pbank[B3]), reads=[pk(B3)], writes=[kk('vb')])
        yield
        acol = C_AF if dirn == 0 else C_AB
        P.op('pe', lambda e: e.matmul(out=pbank[B1], lhsT=cst[:, acol:acol + 128], rhs=lf[:, hsl], start=True, stop=True), reads=[('lf', dirn), 'cst'], writes=[pk(B1)])
        for h in range(4):
            P.op('pe', lambda e, h=h: e.matmul(out=pbank[B2][:, h * 2:(h + 1) * 2], lhsT=lf[:, dirn * 512 + h * 128:dirn * 512 + (h + 1) * 128], rhs=cst[:, C_CH:C_CH + 2],
                                               start=True, stop=True),
                 reads=[('lf', dirn), 'cst'], writes=[pk(B2)])
        yield
        P.op('act', lambda e: e.activation(out=Em[:, hsl], in_=pbank[B1], func=AF.Exp, scale=-1.0), reads=[pk(B1)], writes=[('Em', dirn)])
        P.op('dve', lambda e: e.tensor_tensor(out=kib_, in0=kf[:, hsl], in1=Em[:, hsl], op=ALU.mult), reads=[('kf', dirn), ('Em', dirn)], writes=[kk('kib')])
        P.op('act', lambda e: e.activation(out=expm_.rearrange("p a b -> p (a b)"), in_=pbank[B2][:, 0:8], func=AF.Exp, scale=0.5), reads=[pk(B2)], writes=[kk('expm')])
        P.op('act', lambda e: e.activation(out=decay_.rearrange("p a b -> p (a b)"), in_=pbank[B2][:, 0:8], func=AF.Exp), reads=[pk(B2)], writes=[kk('decay')])
        yield
        DB = [B1, B2]
        for ch in range(2):
            for h in range(4):
                P.op('pe', lambda e, ch=ch, h=h: e.matmul(out=pbank[DB[ch]][:, h * 128:(h + 1) * 128], lhsT=kib_[ch * 64:(ch + 1) * 64, h * 128:(h + 1) * 128],
                                                          rhs=vb_[ch * 64:(ch + 1) * 64, h * 128:(h + 1) * 128], start=True, stop=True),
                     reads=[kk('kib'), kk('vb')], writes=[pk(DB[ch])])
        yield
        Sr = Srun[:, dirn, :]
        Sr3 = Sr.rearrange("p (h e) -> p h e", h=4)
        skeys = [('Srun', dirn, h) for h in range(4)]
        for ch in ([0, 1] if dirn == 0 else [1, 0]):
            em_bc = expm_[:, :, ch:ch + 1].to_broadcast([128, 4, 128])
            if kind == 'l':
                n = ti * 2 + ch
                wi = dirn * 2 + ch
                P.op('pool', lambda e, wi=wi, em_bc=em_bc: e.tensor_tensor(out=sstw[wi].rearrange("p (h e) -> p h e", h=4), in0=Sr3, in1=em_bc, op=ALU.mult),
                     reads=skeys + [kk('expm')], writes=[('sstw', wi)])
                P.dma('sp', sst_d[b, dirn, n], sstw[wi], reads=[('sstw', wi)], writes=[('sst_d', b, dirn, n)], sem=('sstw', wi))
            for h in range(4):
                hs = slice(h * 128, (h + 1) * 128)
                P.op('act', lambda e, ch=ch, h=h, hs=hs: e.activation(out=tmpS_[:, hs], in_=pbank[DB[ch]][:, hs], func=AF.Identity, scale=expm_[:, h, ch:ch + 1], bias=0.0),
                     reads=[pk(DB[ch]), kk('expm')], writes=[(kk('tmpS'), h)])
            for h in range(4):
                hs = slice(h * 128, (h + 1) * 128)
                P.op('dve', lambda e, ch=ch, h=h, hs=hs: e.scalar_tensor_tensor(out=Sr[:, hs], in0=Sr[:, hs], scalar=decay_[:, h, ch:ch + 1], in1=tmpS_[:, hs],
                                                                             op0=ALU.mult, op1=ALU.add),
                     reads=[skeys[h], kk('decay'), (kk('tmpS'), h)], writes=[skeys[h]])
        yield

    pieces = []
    ring_state = {'issued': 0, 'used': 0}

    def ring_issue_upto(n):
        while ring_state['issued'] < min(n, len(pieces)):
            i = ring_state['issued']
            pieces[i](i % NSLOT)
            ring_state['issued'] += 1

    def ring_next(hold=0):
        i = ring_state['used']
        ring_issue_upto(i + NSLOT - hold)
        ring_state['used'] += 1
        return i % NSLOT

    def ld(slot, view, src, rkey):
        P.dma('sp', view, src, reads=[rkey], writes=[('ring', slot)], sem=('ring', slot))

    def v3(slot, off, k, c):
        return ring[slot][:, off:off + k * c].rearrange("p (k c) -> p k c", k=k)

    def group_pieces():
        for p_ in (2, 0, 1):
            pieces.append(lambda s, p_=p_: ld(s, ring[s][:, 0:4096], sB[p_], ('sB', p_)))
        for j in range(8):
            pieces.append(lambda s, j=j: ld(s, ring[s][:, 0:3072], sC[j], ('sC', j)))
        for hf in range(2):
            pieces.append(lambda s, hf=hf: ld(s, ring[s][:, 0:4096], sD[hf], ('sD', hf)))
        for j in range(11):
            pieces.append(lambda s, j=j: ld(s, ring[s][:, 0:4096], sF[j], ('sF', j)))
        for q4 in range(4):
            pieces.append(lambda s, q4=q4: ld(s, ring[s][:, 0:5632], sG[q4], ('sG', q4)))

    for b in range(NB):
        for g in range(NT // G):
            group_pieces()

    def gelu_from_psum(bank, dst, dkey, tmps):
        ps = pbank[bank]
        if use_gelu_tanh:
            P.op('act', lambda e: e.activation(out=dst, in_=ps, func=AF.Gelu_apprx_tanh), reads=[pk(bank)], writes=[dkey])
            return
        t0, t1 = tmps
        P.op('act', lambda e: e.activation(out=t512[t0], in_=ps, func=AF.Square), reads=[pk(bank)], writes=[T5K[t0]])
        P.op('dve', lambda e: e.tensor_scalar(out=t512[t0], in0=t512[t0], scalar1=0.044715, scalar2=1.0, op0=ALU.mult, op1=ALU.add), reads=[T5K[t0]], writes=[T5K[t0]])
        P.op('dve', lambda e: e.tensor_tensor(out=t512[t0], in0=t512[t0], in1=ps, op=ALU.mult), reads=[T5K[t0], pk(bank)], writes=[T5K[t0]])
        P.op('act', lambda e: e.activation(out=t512[t1], in_=t512[t0], func=AF.Sigmoid, scale=1.5957691216057308), reads=[T5K[t0]], writes=[T5K[t1]])
        P.op('dve', lambda e: e.tensor_tensor(out=dst, in0=t512[t1], in1=ps, op=ALU.mult), reads=[T5K[t1], pk(bank)], writes=[dkey])

    def load_xg(b, g):
        for jj in range(G):
            j = g * G + jj
            P.dma('sp', xgs[g % 2][:, jj, :], x[b * L + j * 128: b * L + (j + 1) * 128, :], writes=[('xg', g % 2, jj)], sem=('xg', g % 2, jj))

    def p2_group(b, g, fronts_done=False):
        xg = xgs[g % 2]
        xgk = lambda jj: ('xg', g % 2, jj)
        def stageA_tile(jj):
            j = g * G + jj
            tsl = slice(jj * 128, (jj + 1) * 128)
            hTt = hTg[:, :, tsl]
            hk = ('hTg', jj)
            xk = xgk(jj)
            sstr = sstr2[jj % 2]
            vbA = vbA2[jj % 2]
            vbk = ('vbA', jj % 2)
            sk_ = ('sstr', jj % 2)
            if not fronts_done:
                yield from front(None, xg[:, jj, :], xk, lambda blk: A1[:, blk, b:b + 1], lambda blk: modc[:, blk, b:b + 1], hTt, hk, load=False, slot=2 + jj % 2, bank=0)
            for dirn in range(2):
                for ch in range(2):
                    P.dma('sp', sstr[:, dirn, ch, :], sst_d[b, dirn, j * 2 + ch], reads=[('sst_d', b, dirn, j * 2 + ch)], writes=[sk_], sem=sk_)
            proj_tok(1, hTt, hk, lambda k: win_h[:, k, 0:512], 'win_h')
            proj_tok(2, hTt, hk, lambda k: win_h[:, k, 512:1024], 'win_h')
            ffront([1, 2], 2, 0)
            proj_tok(3, hTt, hk, lambda k: win_h[:, k, 1024:1536], 'win_h')
            P.op('dve', lambda e: e.tensor_copy(out=vbA, in_=pbank[3]), reads=[pk(3)], writes=[vbk])
            yield
            for h in range(4):
                for k in range(8):
                    P.op('pe', lambda e, h=h, k=k, hTt=hTt: e.matmul(out=pbank[1][:, h * 128:(h + 1) * 128], lhsT=win_h[:, k, 1536 + h * 128:1536 + (h + 1) * 128], rhs=hTt[:, k, :],
                                                            start=(k == 0), stop=(k == 7)), reads=HK(hk) + ['win_h'], writes=[pk(1)])
            for blk in range(8):
                P.op('pe', lambda e, blk=blk: e.transpose(out=pbank[6 + blk // 4][:, (blk % 4) * 128:(blk % 4 + 1) * 128], in_=kf[:, blk * 128:(blk + 1) * 128], identity=identf),
                     reads=[('kf', blk // 4), 'cst'], writes=[pk(6 + blk // 4)])
            for blk in range(8):
                acol = C_AF if blk < 4 else C_AB
                P.op('pe', lambda e, blk=blk, acol=acol: e.matmul(out=pbank[4 + blk // 4][:, (blk % 4) * 128:(blk % 4 + 1) * 128], lhsT=lf[:, blk * 128:(blk + 1) * 128],
                                                                  rhs=cst[:, acol:acol + 128], start=True, stop=True),
                     reads=[('lf', blk // 4), 'cst'], writes=[pk(4 + blk // 4)])
            yield
            for d2 in range(2):
                sl = slice(d2 * 512, (d2 + 1) * 512)
                P.op('act', lambda e, d2=d2, sl=sl: e.activation(out=Ep[:, sl], in_=pbank[4 + d2], func=AF.Exp), reads=[pk(4 + d2)], writes=[('Ep', d2)])
                P.op('act', lambda e, d2=d2, sl=sl: e.activation(out=Em[:, sl], in_=pbank[4 + d2], func=AF.Exp, scale=-1.0), reads=[pk(4 + d2)], writes=[('Em', d2)])
                P.op('dve', lambda e, d2=d2, sl=sl: e.tensor_tensor(out=kiT[:, d2 * 4:(d2 + 1) * 4, :].rearrange("p a b -> p (a b)"), in0=pbank[6 + d2], in1=Em[:, sl], op=ALU.mult),
                     reads=[pk(6 + d2), ('Em', d2)], writes=[('kiT', d2)])
                P.op('dve', lambda e, d2=d2, sl=sl: e.tensor_tensor(out=qiT[:, d2 * 4:(d2 + 1) * 4, :].rearrange("p a b -> p (a b)"), in0=pbank[1], in1=Ep[:, sl], op=ALU.mult),
                     reads=[pk(1), ('Ep', d2)], writes=[('qiT', d2)])
            yield
            for d2 in range(2):
                for h in range(4):
                    P.op('pe', lambda e, d2=d2, h=h: e.matmul(out=pbank[2 + d2][:, h * 128:(h + 1) * 128], lhsT=kiT[:, d2 * 4 + h, :], rhs=qiT[:, d2 * 4 + h, :], start=True, stop=True),
                         reads=[('kiT', d2), ('qiT', d2)], writes=[pk(2 + d2)])
                P.op('dve', lambda e, d2=d2: e.tensor_tensor(out=scm[:, d2, :, :], in0=pbank[2 + d2].rearrange("p (h t) -> p h t", h=4),
                                                            in1=maskb[:, d2:d2 + 1, :].to_broadcast([128, 4, 128]), op=ALU.mult),
                     reads=[pk(2 + d2), 'maskb'], writes=[('scm', d2)])
            yield
            for h in range(4):
                ob_ = pbank[4][:, h * 128:(h + 1) * 128]
                P.op('pe', lambda e, h=h, ob_=ob_: e.matmul(out=ob_, lhsT=scm[:, 0, h, :], rhs=vbA[:, h * 128:(h + 1) * 128], start=True, stop=False),
                     reads=[('scm', 0), vbk], writes=[pk(4)])
                for d2 in range(2):
                    for ch in range(2):
                        P.op('pe', lambda e, h=h, d2=d2, ch=ch, sstr=sstr: e.matmul(out=pbank[4][ch * 64:(ch + 1) * 64, h * 128:(h + 1) * 128],
                                                                        lhsT=qiT[:, d2 * 4 + h, ch * 64:(ch + 1) * 64],
                                                                        rhs=sstr[:, d2, ch, h * 128:(h + 1) * 128], start=False, stop=False, skip_group_check=True),
                             reads=[('qiT', d2), sk_], writes=[pk(4)])
                P.op('pe', lambda e, h=h, ob_=ob_: e.matmul(out=ob_, lhsT=scm[:, 1, h, :], rhs=vbA[:, h * 128:(h + 1) * 128], start=False, stop=True),
                     reads=[('scm', 1), vbk], writes=[pk(4)])
            yield
            for h in range(4):
                P.op('act', lambda e, h=h, jj=jj: e.activation(out=oa[:, jj, h * 128:(h + 1) * 128], in_=pbank[4][:, h * 128:(h + 1) * 128], func=AF.Square, accum_out=rs4[:, h:h + 1]),
                     reads=[pk(4)], writes=[('oa', jj), ('rs4', h)])
            P.op('act', lambda e: e.activation(out=rs4[:, 4:8], in_=rs4[:, 0:4], func=AF.Ln, scale=1.0 / 128, bias=EPS_AP), reads=[('rs4', h) for h in range(4)] + ['eps'], writes=['rs4b'])
            P.op('act', lambda e: e.activation(out=rs4[:, 4:8], in_=rs4[:, 4:8], func=AF.Exp, scale=-0.5), reads=['rs4b'], writes=['rs4b'])
            P.op('dve', lambda e, jj=jj: e.tensor_tensor(out=oa[:, jj, :].rearrange("p (h e) -> p h e", h=4), in0=pbank[4].rearrange("p (h e) -> p h e", h=4),
                                                        in1=rs4[:, 4:8].unsqueeze(2).to_broadcast([128, 4, 128]), op=ALU.mult),
                 reads=[pk(4), 'rs4b'], writes=[('oa', jj)])
            yield
        sV = ring_next()
        wvV = v3(sV, 0, 8, 512)

        def bv_tile(jj):
            pbv = 0
            proj_tok(pbv, hTg[:, :, jj * 128:(jj + 1) * 128], ('hTg', jj), lambda k: wvV[:, k, :], ('ring', sV))
            gelu_from_psum(pbv, gu[:, jj, :], ('gu', jj), (0, 1))
            yield
            gv = gu[:, jj, :]
            gvk = ('gu', jj)
            P.op('dve', lambda e, gv=gv: e.bn_stats(out=bnst[:, 0:6], in_=gv), reads=[gvk], writes=['bnst'])
            P.op('dve', lambda e: e.bn_aggr(out=bnst[:, 6:8], in_=bnst[:, 0:6]), reads=['bnst'], writes=['bnst2'])
            P.op('act', lambda e: e.activation(out=ss[:, 12:13], in_=bnst[:, 7:8], func=AF.Ln, bias=EPS_AP, scale=1.0), reads=['bnst2', 'eps'], writes=['ss4'])
            P.op('act', lambda e: e.activation(out=ss[:, 13:14], in_=ss[:, 12:13], func=AF.Exp, scale=-0.5), reads=['ss4'], writes=['ss5'])
            P.op('dve', lambda e, gv=gv: e.tensor_scalar(out=gv, in0=gv, scalar1=bnst[:, 6:7], scalar2=ss[:, 13:14], op0=ALU.subtract, op1=ALU.mult),
                 reads=[gvk, 'bnst2', 'ss5'], writes=[gvk])
            P.op('pool', lambda e, gv=gv: e.tensor_tensor(out=gv, in0=gv, in1=lng, op=ALU.mult), reads=[gvk, 'lng'], writes=[gvk])
            yield
        drive([stageA_tile(jj) for jj in range(G)] + [bv_tile(jj) for jj in range(G)], A_SKEW)
        s = ring_next()
        wv = v3(s, 0, 8, 512)
        for jj in range(G):
            tsl = slice(jj * 128, (jj + 1) * 128)
            pb_ = 1 + jj % 2
            st_ = t512[jj % 2]
            stk = T5K[jj % 2]
            proj_tok(pb_, hTg[:, :, tsl], ('hTg', jj), lambda k: wv[:, k, :], ('ring', s))
            P.op('act', lambda e, pb_=pb_, st_=st_: e.activation(out=st_, in_=pbank[pb_], func=AF.Silu), reads=[pk(pb_)], writes=[stk])
            P.op('dve', lambda e, jj=jj, st_=st_: e.tensor_tensor(out=oag2[jj % 2], in0=oa[:, jj, :], in1=st_, op=ALU.mult), reads=[('oa', jj), stk], writes=[('oag', jj % 2)])
        for jj in range(G):
            tb_ = 3 + jj % 2
            for h in range(4):
                P.op('pe', lambda e, h=h, tb_=tb_, jj=jj: e.transpose(out=pbf[tb_][:, h * 128:(h + 1) * 128], in_=oag2[jj % 2][:, h * 128:(h + 1) * 128], identity=identb),
                     reads=[('oag', jj % 2), 'identb'], writes=[pk(tb_)])
        for jj in range(G):
            tsl = slice(jj * 128, (jj + 1) * 128)
            tb_ = 3 + jj % 2
            P.op('act', lambda e, tsl=tsl, tb_=tb_: e.activation(out=oaT[:, :, tsl], in_=pbf[tb_][:, 0:512].rearrange("p (h t) -> p h t", h=4), func=AF.Identity,
                                                        scale=vc[:, V_GNA:V_GNA + 1], bias=0.0), reads=[pk(tb_), 'vc'], writes=[('oaT', jj)])
        s = ring_next()
        wv = v3(s, 0, 8, 512)
        for jj in range(G):
            tsl = slice(jj * 128, (jj + 1) * 128)
            pb_ = 5 + jj % 2
            proj_tok(pb_, hTg[:, :, tsl], ('hTg', jj), lambda k: wv[:, k, :], ('ring', s))
            gelu_from_psum(pb_, t512[2 + jj % 2], T5K[2 + jj % 2], (0, 1))
        hkeys = [kx for jj in range(G) for kx in HK(('hTg', jj))]
        for jj in range(G):
            mb_ = [7, 6][jj % 2]
            gv = gu[:, jj, :]
            gvk = ('gu', jj)
            ug = t512[2 + jj % 2]
            ugk = T5K[2 + jj % 2]
            vnb_ = vnb2[jj % 2]
            obm_ = obm2[jj % 2]
            P.op('pool', lambda e, gv=gv, vnb_=vnb_: e.tensor_tensor(out=vnb_, in0=gv, in1=lnb, op=ALU.add), reads=[gvk, 'lnb'], writes=[('vnb', jj % 2)])
            for g4 in range(4):
                P.op('pe', lambda e, g4=g4, mb_=mb_, vnb_=vnb_: e.matmul(out=pbank[mb_][:, g4 * 128:(g4 + 1) * 128], lhsT=wsT[:, g4, :], rhs=vnb_[:, g4 * 128:(g4 + 1) * 128],
                                                                   start=True, stop=True),
                     reads=['wsT', ('vnb', jj % 2)], writes=[pk(mb_)])
            for g4 in range(4):
                P.op('dve', lambda e, g4=g4, ug=ug, mb_=mb_, obm_=obm_: e.scalar_tensor_tensor(out=obm_[:, g4 * 128:(g4 + 1) * 128], in0=pbank[mb_][:, g4 * 128:(g4 + 1) * 128],
                                                                          scalar=vc[:, V_BS + g4:V_BS + g4 + 1], in1=ug[:, g4 * 128:(g4 + 1) * 128], op0=ALU.add, op1=ALU.mult),
                     reads=[pk(mb_), 'vc', ugk], writes=[('obm', jj % 2, g4)])

        def c_gates(jb):
            s = ring_next()
            wga = v3(s, 0, 8, 128)
            wgb = v3(s, 1024, 8, 128)
            rk = ('ring', s)
            o_ = 4 * (jb % 2)
            for k in range(8):
                P.op('pe', lambda e, k=k: e.matmul(out=pbank[o_ + 0][:, 0:T], lhsT=wga[:, k, :], rhs=hTg[:, k, :], start=(k == 0), stop=(k == 7)), reads=hkeys + [rk], writes=[pk(o_ + 0)])
            for k in range(8):
                P.op('pe', lambda e, k=k: e.matmul(out=pbank[o_ + 1][:, 0:T], lhsT=wgb[:, k, :], rhs=hTg[:, k, :], start=(k == 0), stop=(k == 7)), reads=hkeys + [rk], writes=[pk(o_ + 1)])
            return s

        def c_rest(jb, s):
            wpa_ = v3(s, 2048, 4, 128)
            wpb_ = v3(s, 2560, 4, 128)
            rk = ('ring', s)
            o_ = 4 * (jb % 2)
            gts = gtsets[jb % 2]
            gks = gtkeys[jb % 2]
            for k in range(4):
                P.op('pe', lambda e, k=k: e.matmul(out=pbank[o_ + 2][:, 0:T], lhsT=wpa_[:, k, :], rhs=oaT[:, k, :], start=(k == 0), stop=(k == 3)),
                     reads=[('oaT', jj) for jj in range(G)] + [rk], writes=[pk(o_ + 2)])
            for k in range(4):
                P.op('pe', lambda e, k=k: e.matmul(out=pbank[o_ + 3][:, 0:T], lhsT=wpb_[:, k, :], rhs=obT[:, k, :], start=(k == 0), stop=(k == 3)),
                     reads=[('obT', jj) for jj in range(G)] + [rk], writes=[pk(o_ + 3)])
            P.op('act', lambda e: e.activation(out=gts[0], in_=pbank[o_ + 0][:, 0:T], func=AF.Sigmoid), reads=[pk(o_ + 0)], writes=[gks[0]])
            P.op('act', lambda e: e.activation(out=gts[1], in_=pbank[o_ + 1][:, 0:T], func=AF.Sigmoid), reads=[pk(o_ + 1)], writes=[gks[1]])
            P.op('dve', lambda e: e.tensor_tensor(out=gts[2], in0=pbank[o_ + 2][:, 0:T], in1=gts[0], op=ALU.mult), reads=[pk(o_ + 2), gks[0]], writes=[gks[2]])
            P.op('dve', lambda e: e.tensor_tensor(out=gts[3], in0=pbank[o_ + 3][:, 0:T], in1=gts[1], op=ALU.mult), reads=[pk(o_ + 3), gks[1]], writes=[gks[3]])
            P.op('pool', lambda e: e.tensor_tensor(out=mT[:, jb, :], in0=gts[2], in1=gts[3], op=ALU.add), reads=[gks[2], gks[3]], writes=[('mT', jb)])

        s0 = c_gates(0)
        for jj in range(G):
            tsl = slice(jj * 128, (jj + 1) * 128)
            tb_ = 3 + jj % 2
            obm_ = obm2[jj % 2]
            for g4 in range(4):
                P.op('pe', lambda e, g4=g4, tb_=tb_, obm_=obm_: e.transpose(out=pbf[tb_][:, g4 * 128:(g4 + 1) * 128], in_=obm_[:, g4 * 128:(g4 + 1) * 128], identity=identb),
                     reads=[('obm', jj % 2, g4), 'identb'], writes=[pk(tb_)])
            P.op('act', lambda e, tsl=tsl, tb_=tb_: e.activation(out=obT[:, :, tsl], in_=pbf[tb_][:, 0:512].rearrange("p (h t) -> p h t", h=4), func=AF.Copy), reads=[pk(tb_)], writes=[('obT', jj)])
        c_rest(0, s0)
        for jb in range(1, 8):
            c_rest(jb, c_gates(jb))
        mkeys = [('mT', jb) for jb in range(8)]
        sD0 = ring_next()
        sD1 = ring_next(hold=1)
        wvs = [v3(sD0, 0, 8, 512), v3(sD1, 0, 8, 512)]
        wks = [('ring', sD0), ('ring', sD1)]

        def de_tile(jj):
            tsl = slice(jj * 128, (jj + 1) * 128)
            for hf in range(2):
                bk = 2 + hf + 2 * (jj % 2)
                ti_ = 2 + hf
                for k in range(8):
                    P.op('pe', lambda e, k=k, bk=bk, hf=hf: e.matmul(out=pbank[bk], lhsT=mT[:, k, tsl], rhs=wvs[hf][:, k, :], start=(k == 0), stop=(k == 7)),
                         reads=[('mT', k), wks[hf]], writes=[pk(bk)])
                P.op('dve', lambda e, bk=bk, hf=hf, ti_=ti_: e.tensor_tensor(out=t512[ti_], in0=pbank[bk], in1=G2[:, hf * 512:(hf + 1) * 512], op=ALU.mult),
                     reads=[pk(bk), 'G2'], writes=[T5K[ti_]])
                P.op('pool', lambda e, hf=hf, ti_=ti_: e.tensor_tensor(out=xg[:, jj, hf * 512:(hf + 1) * 512], in0=xg[:, jj, hf * 512:(hf + 1) * 512], in1=t512[ti_], op=ALU.add),
                     reads=[xgk(jj), T5K[ti_]], writes=[xgk(jj)])
            yield
            yield from front(None, xg[:, jj, :], xgk(jj), lambda blk: A2[:, blk, b:b + 1], lambda blk: modc[:, 24 + blk, b:b + 1], hTg[:, :, tsl], ('hTg', jj),
                             load=False, slot=2 + jj % 2, bank=6 + jj % 2)
        drive([de_tile(jj) for jj in range(G)], 1)
        if g + 1 < NT // G:
            load_xg(b, g + 1)
        nfr = []
        if g + 1 < NT // G:
            xn_ = xgs[(g + 1) % 2]
            nfr = [front(None, xn_[:, jj, :], ('xg', (g + 1) % 2, jj), lambda blk: A1[:, blk, b:b + 1], lambda blk: modc[:, blk, b:b + 1],
                         hTg[:, :, jj * 128:(jj + 1) * 128], ('hTg', jj), load=False, slot=2 + jj % 2, bank=6 + jj % 2) for jj in range(G)]
        for jf in range(22):
            if jf == 12:
                for fr in nfr:
                    next(fr)
            if jf % 2 == 0:
                s = ring_next()
            wa = v3(s, (jf % 2) * 2048, 8, 128)
            wb_ = v3(s, (jf % 2) * 2048 + 1024, 8, 128)
            rk = ('ring', s)
            ba = 2 * (jf % 3)
            for k in range(8):
                P.op('pe', lambda e, k=k, ba=ba, wa=wa: e.matmul(out=pbank[ba][:, 0:T], lhsT=wa[:, k, :], rhs=hTg[:, k, :], start=(k == 0), stop=(k == 7)), reads=hkeys + [rk], writes=[pk(ba)])
            for k in range(8):
                P.op('pe', lambda e, k=k, ba=ba, wb_=wb_: e.matmul(out=pbank[ba + 1][:, 0:T], lhsT=wb_[:, k, :], rhs=hTg[:, k, :], start=(k == 0), stop=(k == 7)), reads=hkeys + [rk], writes=[pk(ba + 1)])
            gi = jf % 2
            P.op('act', lambda e, ba=ba, gi=gi: e.activation(out=gtF[gi], in_=pbank[ba][:, 0:T], func=AF.Silu), reads=[pk(ba)], writes=[('gtF', gi)])
            P.op('dve', lambda e, ba=ba, gi=gi, jf=jf: e.tensor_tensor(out=actT[:, jf, :], in0=pbank[ba + 1][:, 0:T], in1=gtF[gi], op=ALU.mult), reads=[pk(ba + 1), ('gtF', gi)], writes=[('actT', jf)])
        akeys = [('actT', jf) for jf in range(22)]
        if nfr:
            for fr in nfr:
                next(fr)
        for q4 in range(4):
            if q4 == 1:
                for fr in nfr:
                    for _ in fr:
                        pass
            s = ring_next()
            wv = v3(s, 0, 22, 256)
            for jj in range(G):
                tsl = slice(jj * 128, (jj + 1) * 128)
                bk = 4 + (jj % 2)
                for k in range(22):
                    P.op('pe', lambda e, k=k, tsl=tsl, bk=bk, wv=wv: e.matmul(out=pbank[bk][:, 0:256], lhsT=actT[:, k, tsl], rhs=wv[:, k, :], start=(k == 0), stop=(k == 21)),
                         reads=[('actT', k), ('ring', s)], writes=[pk(bk)])
                P.op('dve', lambda e, bk=bk, q4=q4: e.tensor_tensor(out=t512[3][:, 0:256], in0=pbank[bk][:, 0:256], in1=G5[:, q4 * 256:(q4 + 1) * 256], op=ALU.mult),
                     reads=[pk(bk), 'G5'], writes=[T5K[3]])
                P.op('pool', lambda e, jj=jj, q4=q4: e.tensor_tensor(out=xg[:, jj, q4 * 256:(q4 + 1) * 256], in0=xg[:, jj, q4 * 256:(q4 + 1) * 256], in1=t512[3][:, 0:256], op=ALU.add),
                     reads=[xgk(jj), T5K[3]], writes=[xgk(jj)])
        for jj in range(G):
            j = g * G + jj
            pj = jj % 2
            c0 = 16 + 3 * pj
            kH = [('ssH', pj, i) for i in range(3)]
            P.op('act', lambda e, jj=jj, pj=pj, c0=c0: e.activation(out=xn2[pj], in_=xg[:, jj, :], func=AF.Square, accum_out=ss[:, c0:c0 + 1]),
                 reads=[xgk(jj)], writes=[kH[0], ('xn', pj)])
            P.op('act', lambda e, c0=c0: e.activation(out=ss[:, c0 + 1:c0 + 2], in_=ss[:, c0:c0 + 1], func=AF.Ln, scale=1.0 / D, bias=EPS_AP), reads=[kH[0], 'eps'], writes=[kH[1]])
            P.op('act', lambda e, c0=c0: e.activation(out=ss[:, c0 + 2:c0 + 3], in_=ss[:, c0 + 1:c0 + 2], func=AF.Exp, scale=-0.5), reads=[kH[1]], writes=[kH[2]])
            P.op('dve', lambda e, jj=jj, c0=c0: e.scalar_tensor_tensor(out=xg[:, jj, :], in0=xg[:, jj, :], scalar=ss[:, c0 + 2:c0 + 3], in1=gfin, op0=ALU.mult, op1=ALU.mult),
                 reads=[xgk(jj), kH[2], 'gfin'], writes=[xgk(jj)])
            P.dma('sp', out[b * L + j * 128: b * L + (j + 1) * 128, :], xg[:, jj, :], reads=[xgk(jj)], sem=('yo', pj))

    P.barrier()
    for b in range(NB):
        bcast_tile(G2, lambda b=b: modc[:, 16:24, b], ['modc'], 'G2')
        bcast_tile(G5, lambda b=b: modc[:, 40:48, b], ['modc'], 'G5')
        P.op('pool', lambda e: e.memset(Srun.rearrange("p a b -> p (a b)"), 0.0), writes=[('Srun', d_, h_) for d_ in range(2) for h_ in range(4)])
        seqf = [('c', i) for i in range(NCT)] + [('l', i) for i in range(NT)]
        seqb = [('c', i) for i in reversed(range(NCT))] + [('l', i) for i in reversed(range(NT))]
        load_xg(b, 0)
        gens = []
        for step in range(NCT + NT):
            gens.append(p1_tile(b, 0, seqf[step][0], seqf[step][1], step))
            gens.append(p1_tile(b, 1, seqb[step][0], seqb[step][1], step))
        if b == 0:
            drive_first([late_precast_a(), late_precast_b()], gens, P1_SKEW)
        else:
            drive(gens, P1_SKEW, per_start=2)
        for g in range(NT // G):
            p2_group(b, g, fronts_done=(g > 0))
    P.finish()
    P.replay()
    return nc


_CACHE = {}


def _core_inputs(inp, b0, NB, L):
    f = lambda a: np.ascontiguousarray(np.asarray(a, dtype=np.float32))
    x = f(inp['x'])[b0:b0 + NB].reshape(NB * L, D)
    ctx = f(inp['ctx'])[b0:b0 + NB].reshape(NB * CTX, D)
    cvec = np.concatenate([f(inp['c'])[b0:b0 + NB], f(inp['c_ctx'])[None, :]], 0)
    if NB < 4:
        cvec = np.concatenate([cvec[:NB], np.zeros((4 - NB, D), np.float32), cvec[NB:]], 0)
    vrows = np.concatenate([
        f(inp['b_mod'])[0].reshape(48, 128), f(inp['g_mix'])[0].reshape(8, 128), f(inp['g_ffn'])[0].reshape(8, 128),
        f(inp['lb_gamma']).reshape(16, 128), f(inp['g_norm_a'])[0].reshape(1, 128), f(inp['b_s'])[0].reshape(4, 128)], 0)
    return {
        'x': x, 'ctx': ctx, 'cvec': np.ascontiguousarray(cvec), 'w_mod': f(inp['w_mod'])[0], 'vrows': np.ascontiguousarray(vrows),
        'w_in': f(inp['w_in'])[0], 'ln_v_g': f(inp['ln_v_g'])[0], 'ln_v_b': f(inp['ln_v_b'])[0], 'w_s': f(inp['w_s'])[0],
        'w_pa': f(inp['w_pa'])[0], 'w_pb': f(inp['w_pb'])[0], 'w_o': f(inp['w_o'])[0], 'w_up': f(inp['w_up'])[0],
        'w_down': f(inp['w_down'])[0], 'g_final': f(inp['g_final']), 'cst': make_consts(),
    }


def run(inp, ncores, G=2, use_gelu_tanh=True):
    import os
    use_gelu_tanh = use_gelu_tanh and os.environ.get('GT', '1') == '1'
    B, L = inp['x'].shape[0], inp['x'].shape[1]
    NB = B // ncores
    NT = L // 128
    key = (NB, NT, G, use_gelu_tanh)
    import os
    nc = build(NB, NT, G, use_gelu_tanh, P1_SKEW=int(os.environ.get('P1S', '3')), A_SKEW=int(os.environ.get('AS', '4')))
    in_maps = [_core_inputs(inp, i * NB, NB, L) for i in range(ncores)]
    res = run_bass_kernel_spmd(nc, in_maps, core_ids=list(range(ncores)))
    outs = [np.asarray(r['out']).reshape(NB, L, D) for r in res.results]
    return np.concatenate(outs, 0).astype(np.float32)


def kernel(**inputs):
    return run(inputs, NCORES, G=2)
```
